# Optimizing a Trainium2 kernel written in Bass

```python
import jax, jax.numpy as jnp
from jax import lax
import numpy as np

D_MODEL = 2048
BATCH = 16
SEQ = 2048
DEPTH = 1
DEC_BATCH = 4
DEC_SEQ = 2048
PAST_LEN = 128

D_FF = 5632
D_CONV = 1024
CONV_WIDTH = 3
N_GROUPS = 3
HEADS_PER_GROUP = 4
N_HEADS = N_GROUPS * HEADS_PER_GROUP
HEAD_DIM = 128
D_ATTN = HEADS_PER_GROUP * HEAD_DIM
D_QKV = N_GROUPS * D_ATTN
ATTN_WINDOWS = (128, 512, 2048)
ATTN_DILATIONS = (1, 4, 16)
N_BRANCHES = 2
D_IN = 3 * D_CONV + 3 * D_QKV
RMS_EPS = 1e-6
FFN_RESIDUAL_WEIGHT = 0.5

kernel_name = "hybrid_conv_dilated_attn_macaron_encoder"


def rms_norm(x, g):
    xf = x.astype(jnp.float32)
    y = xf * lax.rsqrt(jnp.mean(xf * xf, axis=-1, keepdims=True) + RMS_EPS)
    return (y * g.astype(jnp.float32)).astype(x.dtype)


def swiglu(x, w1, w3, w2):
    a = jnp.einsum('bsd,df->bsf', x, w1)
    c = jnp.einsum('bsd,df->bsf', x, w3)
    return jnp.einsum('bsf,fd->bsd', jax.nn.silu(a) * c, w2)


def alibi_slopes():
    i = jnp.arange(1, N_HEADS + 1, dtype=jnp.float32)
    return jnp.exp2(-8.0 * i / N_HEADS).reshape(N_GROUPS, HEADS_PER_GROUP)


def dilated_band_attention(q, k, v, dilation, radius, slopes):
    b, s, h, hd = q.shape
    L = s // dilation

    def to_sub(t):
        return t.reshape(b, L, dilation, h, hd).transpose(0, 2, 1, 3, 4).reshape(b * dilation, L, h, hd)

    qs, ks, vs = to_sub(q), to_sub(k), to_sub(v)
    blk = radius
    nb = -(-L // blk)
    lp = nb * blk
    qb = jnp.pad(qs, ((0, 0), (0, lp - L), (0, 0), (0, 0))).reshape(-1, nb, blk, h, hd)

    def windows(t):
        tb = jnp.pad(t, ((0, 0), (blk, lp - L + blk), (0, 0), (0, 0))).reshape(-1, nb + 2, blk, h, hd)
        return jnp.concatenate([tb[:, :-2], tb[:, 1:-1], tb[:, 2:]], axis=2)

    kw, vw = windows(ks), windows(vs)
    scores = jnp.einsum('nbqhd,nbkhd->nbhqk', qb, kw).astype(jnp.float32) * (HEAD_DIM ** -0.5)
    q_pos = jnp.arange(nb)[:, None] * blk + jnp.arange(blk)[None, :]
    k_pos = (jnp.arange(nb)[:, None] - 1) * blk + jnp.arange(3 * blk)[None, :]
    delta = jnp.abs(k_pos[:, None, :] - q_pos[:, :, None])
    valid = (delta <= radius) & (k_pos[:, None, :] >= 0) & (k_pos[:, None, :] < L)
    dist = (dilation * delta).astype(jnp.float32)
    bias = -slopes.astype(jnp.float32)[None, :, None, None] * dist[:, None]
    scores = jnp.where(valid[:, None][None], scores + bias[None], -jnp.inf)
    m = jnp.max(scores, axis=-1, keepdims=True)
    p = jnp.exp(scores - m)
    l = jnp.sum(p, axis=-1, keepdims=True)
    o = jnp.einsum('nbhqk,nbkhd->nbqhd', (p / l).astype(v.dtype), vw)
    lse = (m + jnp.log(l))[..., 0]
    o = o.reshape(b, dilation, lp, h, hd)[:, :, :L].transpose(0, 2, 1, 3, 4).reshape(b, s, h, hd)
    lse = lse.transpose(0, 1, 3, 2).reshape(b, dilation, lp, h)[:, :, :L].transpose(0, 2, 1, 3).reshape(b, s, h)
    return o, lse


def token_mixer(u, w_in, conv_w, conv_b, w_conv_out, w_attn_out, w_gate, b_gate, w_o):
    b, s, _ = u.shape
    proj = jnp.einsum('bsd,de->bse', u, w_in)
    c_b, c_c, c_x, q, k, v = jnp.split(
        proj, [D_CONV, 2 * D_CONV, 3 * D_CONV, 3 * D_CONV + D_QKV, 3 * D_CONV + 2 * D_QKV], axis=-1)

    z = c_c * c_x
    half = CONV_WIDTH // 2
    zp = jnp.pad(z, ((0, 0), (half, half), (0, 0)))
    conv = sum(zp[:, j:j + s] * conv_w[j] for j in range(CONV_WIDTH)) + conv_b
    y_conv = jnp.einsum('bsc,cd->bsd', c_b * conv, w_conv_out)

    q = q.reshape(b, s, N_GROUPS, HEADS_PER_GROUP, HEAD_DIM)
    k = k.reshape(b, s, N_GROUPS, HEADS_PER_GROUP, HEAD_DIM)
    v = v.reshape(b, s, N_GROUPS, HEADS_PER_GROUP, HEAD_DIM)
    slopes = alibi_slopes()
    outs, lses = [], []
    for g in range(N_GROUPS):
        d = ATTN_DILATIONS[g]
        o_g, lse_g = dilated_band_attention(q[:, :, g], k[:, :, g], v[:, :, g], d,
                                            ATTN_WINDOWS[g] // (2 * d), slopes[g])
        outs.append(o_g)
        lses.append(lse_g)
    alpha = jax.nn.softmax(jnp.stack(lses, axis=0), axis=0)
    o = jnp.sum(alpha[..., None].astype(v.dtype) * jnp.stack(outs, axis=0), axis=0).reshape(b, s, D_ATTN)
    y_attn = jnp.einsum('bse,ed->bsd', o, w_attn_out)

    gates = jax.nn.sigmoid(jnp.einsum('bsd,de->bse', u, w_gate) + b_gate).reshape(b, s, N_BRANCHES, D_MODEL)
    merged = gates[:, :, 0] * y_conv + gates[:, :, 1] * y_attn
    return jnp.einsum('bsd,de->bse', merged, w_o)


def encoder_layer(x, ffn1_norm_pre, ffn1_w1, ffn1_w3, ffn1_w2, ffn1_norm_post,
                  mix_norm_pre, w_in, conv_w, conv_b, w_conv_out, w_attn_out, w_gate, b_gate, w_o,
                  mix_norm_post, ffn2_norm_pre, ffn2_w1, ffn2_w3, ffn2_w2, ffn2_norm_post):
    h = x + FFN_RESIDUAL_WEIGHT * rms_norm(swiglu(rms_norm(x, ffn1_norm_pre), ffn1_w1, ffn1_w3, ffn1_w2), ffn1_norm_post)
    mix = token_mixer(rms_norm(h, mix_norm_pre), w_in, conv_w, conv_b, w_conv_out, w_attn_out, w_gate, b_gate, w_o)
    h = h + rms_norm(mix, mix_norm_post)
    return h + FFN_RESIDUAL_WEIGHT * rms_norm(swiglu(rms_norm(h, ffn2_norm_pre), ffn2_w1, ffn2_w3, ffn2_w2), ffn2_norm_post)


def setup_inputs(seed: int = 0) -> dict:
    key = jax.random.key(seed)
    ks = jax.random.split(key, 24)

    def nrm(k, shape, fan_in):
        return jax.random.normal(k, shape, jnp.float32) * (fan_in ** -0.5)

    def gain(k):
        return 1.0 + 0.05 * jax.random.normal(k, (DEPTH, D_MODEL), jnp.float32)

    return {
        "x_prompt": jax.random.normal(ks[0], (BATCH, SEQ, D_MODEL), jnp.float32),
        "x_sample": jax.random.normal(ks[1], (DEC_BATCH, DEC_SEQ, D_MODEL), jnp.float32),
        "ffn1_norm_pre": gain(ks[2]),
        "ffn1_w1": nrm(ks[3], (DEPTH, D_MODEL, D_FF), D_MODEL),
        "ffn1_w3": nrm(ks[4], (DEPTH, D_MODEL, D_FF), D_MODEL),
        "ffn1_w2": nrm(ks[5], (DEPTH, D_FF, D_MODEL), D_FF),
        "ffn1_norm_post": gain(ks[6]),
        "mix_norm_pre": gain(ks[7]),
        "w_in": nrm(ks[8], (DEPTH, D_MODEL, D_IN), D_MODEL),
        "conv_w": nrm(ks[9], (DEPTH, CONV_WIDTH, D_CONV), CONV_WIDTH),
        "conv_b": 0.02 * jax.random.normal(ks[10], (DEPTH, D_CONV), jnp.float32),
        "w_conv_out": nrm(ks[11], (DEPTH, D_CONV, D_MODEL), D_CONV),
        "w_attn_out": nrm(ks[12], (DEPTH, D_ATTN, D_MODEL), D_ATTN),
        "w_gate": nrm(ks[13], (DEPTH, D_MODEL, N_BRANCHES * D_MODEL), D_MODEL),
        "b_gate": 0.1 * jax.random.normal(ks[14], (DEPTH, N_BRANCHES * D_MODEL), jnp.float32),
        "w_o": nrm(ks[15], (DEPTH, D_MODEL, D_MODEL), D_MODEL),
        "mix_norm_post": gain(ks[16]),
        "ffn2_norm_pre": gain(ks[17]),
        "ffn2_w1": nrm(ks[18], (DEPTH, D_MODEL, D_FF), D_MODEL),
        "ffn2_w3": nrm(ks[19], (DEPTH, D_MODEL, D_FF), D_MODEL),
        "ffn2_w2": nrm(ks[20], (DEPTH, D_FF, D_MODEL), D_FF),
        "ffn2_norm_post": gain(ks[21]),
    }


def reference(x_prompt, x_sample, ffn1_norm_pre, ffn1_w1, ffn1_w3, ffn1_w2, ffn1_norm_post,
              mix_norm_pre, w_in, conv_w, conv_b, w_conv_out, w_attn_out, w_gate, b_gate, w_o,
              mix_norm_post, ffn2_norm_pre, ffn2_w1, ffn2_w3, ffn2_w2, ffn2_norm_post):
    weights = (ffn1_norm_pre, ffn1_w1, ffn1_w3, ffn1_w2, ffn1_norm_post,
               mix_norm_pre, w_in, conv_w, conv_b, w_conv_out, w_attn_out, w_gate, b_gate, w_o,
               mix_norm_post, ffn2_norm_pre, ffn2_w1, ffn2_w3, ffn2_w2, ffn2_norm_post)
    y_prompt = x_prompt
    y_sample = x_sample
    for layer in range(DEPTH):
        lw = [w[layer] for w in weights]
        y_prompt = encoder_layer(y_prompt, *lw)
        y_sample = encoder_layer(y_sample, *lw)
    return (y_prompt, y_sample)
```

```python
import contextlib
import numpy as np
import concourse.bass as bass
import concourse.mybir as mybir
from concourse.bass_utils import run_bass_kernel_spmd

F32 = mybir.dt.float32
BF16 = mybir.dt.bfloat16
ALU = mybir.AluOpType
AF = mybir.ActivationFunctionType

D = 2048
DFF = 5632
SEQ = 2048
T = 512
NT = SEQ // T
NCH = D // 128
NFC = DFF // 128
DCONV = 1024
DQKV = 1536
DIN = 7680
EPS = 1e-6
NCORES = 8
NSEQ_CORE = 3
SLOT_ELEMS = 16 * 512
import os as _os
XQ = _os.environ.get("MK_XQ", "pool")
NSLOT = 3
PF = 2

PV_PRE1, PV_POST1, PV_MPRE, PV_MPOST, PV_PRE2, PV_POST2 = 0, 16, 32, 48, 64, 80
PV_CW0, PV_CW1, PV_CW2, PV_CB, PV_BG = 96, 104, 112, 120, 128
PV_N = 160
ENGS = ("pe", "act", "dve", "pool", "sp")


class Buf:
    __slots__ = ("name", "last_w", "readers", "excl")

    def __init__(self, name, excl=False):
        self.name = name
        self.last_w = None
        self.readers = []
        self.excl = excl


class DSem:
    __slots__ = ("sem", "count", "last_op")

    def __init__(self, sem):
        self.sem = sem
        self.count = 0
        self.last_op = None


class Op:
    __slots__ = ("eng", "meth", "args", "kw", "deps", "is_dma", "dsem", "dval", "signal", "val")

    def __init__(self, eng, meth, args, kw, is_dma=False, dsem=None):
        self.eng = eng
        self.meth = meth
        self.args = args
        self.kw = kw
        self.deps = []
        self.is_dma = is_dma
        self.dsem = dsem
        self.dval = 0
        self.signal = False
        self.val = 0


class Sched:
    def __init__(self):
        self.q = {e: [] for e in ENGS}
        self.dsems = []

    def _dep(self, op, d, kind):
        if d is None or d is op:
            return
        if (not d.is_dma) and (not op.is_dma) and d.eng == op.eng:
            if kind == "raw" and op.eng != "pe":
                op.deps.append(d)
            return
        op.deps.append(d)

    def op(self, eng, meth, *args, reads=(), writes=(), dsem=None, **kw):
        dma = dsem is not None
        o = Op(eng, meth, args, kw, is_dma=dma, dsem=dsem)
        if any(b.excl for b in reads):
            writes = list(writes) + [b for b in reads if b.excl and b not in writes]
            reads = [b for b in reads if not b.excl]
        for b in reads:
            self._dep(o, b.last_w, "raw")
        for b in writes:
            self._dep(o, b.last_w, "waw")
            for r in b.readers:
                self._dep(o, r, "war")
        for b in reads:
            if not dma:
                b.readers = [r for r in b.readers if r.is_dma or r.eng != eng]
            b.readers.append(o)
        for b in writes:
            b.last_w = o
            b.readers = []
        if dma:
            if dsem.last_op is not None:
                o.deps.append(dsem.last_op)
            dsem.count += 16
            o.dval = dsem.count
            dsem.last_op = o
        self.q[eng].append(o)
        return o

    def barrier(self):
        drains = {}
        for e in ENGS:
            o = Op(e, "drain", (), {})
            self.q[e].append(o)
            drains[e] = o
        for e in ENGS:
            o2 = Op(e, "nop", (), {})
            o2.deps = [drains[f] for f in ENGS if f != e] + [ds.last_op for ds in self.dsems if ds.last_op is not None]
            self.q[e].append(o2)

    def emit(self, block, sems):
        for e in self.q:
            for o in self.q[e]:
                for d in o.deps:
                    if not d.is_dma:
                        d.signal = True
        for e in self.q:
            c = 0
            for o in self.q[e]:
                if not o.is_dma and o.signal:
                    c += 1
                    o.val = c
            self.stats = getattr(self, "stats", {})
            self.stats[e] = (len(self.q[e]), c)

        def run(engname, handle):
            seen = {}
            for o in self.q[engname]:
                for d in o.deps:
                    if d.is_dma:
                        s, v = d.dsem.sem, d.dval
                    else:
                        s, v = sems[d.eng], d.val
                    k = id(s)
                    if seen.get(k, 0) >= v:
                        continue
                    seen[k] = v
                    handle.wait_ge(s, v)
                ins = getattr(handle, o.meth)(*o.args, **o.kw)
                if o.is_dma:
                    ins.then_inc(o.dsem.sem, 16)
                elif o.signal:
                    ins.then_inc(sems[engname], 1)

        @block.tensor
        def _(h):
            run("pe", h)

        @block.scalar
        def _(h):
            run("act", h)

        @block.vector
        def _(h):
            run("dve", h)

        @block.gpsimd
        def _(h):
            run("pool", h)

        @block.sync
        def _(h):
            run("sp", h)


def build(nseq=NSEQ_CORE, phases=("P1", "P2", "P3"), ntiles=NT, debug=False):
    NTOK = nseq * SEQ
    nc = bass.Bass("TRN2", target_bir_lowering=False)
    es = contextlib.ExitStack()

    def din(name, shape, dt=F32):
        return nc.dram_tensor(name, list(shape), dt, kind="ExternalInput").ap()

    def dscr(name, shape, dt, out=False):
        return nc.dram_tensor(name, list(shape), dt, kind=("ExternalOutput" if out else "Internal")).ap()

    x_d = din("x", [NTOK, D])
    w_d = {}
    if "P1" in phases:
        w_d.update({"f1w1": din("f1w1", [D, DFF]), "f1w3": din("f1w3", [D, DFF]), "f1w2": din("f1w2", [DFF, D])})
    if "P2" in phases:
        w_d.update({"win": din("win", [D, DIN]), "wgate": din("wgate", [D, 2 * D]),
                    "wco": din("wco", [DCONV, D]), "wao": din("wao", [512, D]), "wo": din("wo", [D, D])})
    if "P3" in phases:
        w_d.update({"f2w1": din("f2w1", [D, DFF]), "f2w3": din("f2w3", [D, DFF]), "f2w2": din("f2w2", [DFF, D])})
    pvec_d = din("pvec", [128, PV_N])
    ident_d = din("ident", [128, 128])
    etab_d = din("etab", [128, 28 * 128])
    y_d = nc.dram_tensor("y", [NTOK, D], F32, kind="ExternalOutput").ap()

    wb = {}
    for k in w_d:
        if k.endswith("w2"):
            wb[k] = dscr(k + "b", [NCH, 128, NFC * 128], BF16)
        else:
            wb[k] = dscr(k + "b", list(w_d[k].shape), BF16)
    Hs = dscr("Hs", [NCH, 128, NTOK], F32, out=debug)
    Us = dscr("Us", [NCH, 128, NTOK], BF16, out=debug)
    H2s = dscr("H2s", [NCH, 128, NTOK], F32, out=debug)
    Os = dscr("Os", [4, 128, NTOK], BF16, out=debug)
    Vs = dscr("Vs", [NTOK, DQKV], BF16, out=debug)

    S = Sched()
    sems = {e: es.enter_context(nc.semaphore("s_" + e)) for e in ENGS}

    def new_dsem():
        ds = DSem(es.enter_context(nc.semaphore("d%d" % len(S.dsems))))
        S.dsems.append(ds)
        return ds

    ARENA_W = 205 * 1024 // 4
    arena = nc.alloc_sbuf_tensor("arena", [128, ARENA_W], F32)

    def region(start):
        st = {"o": start}

        def a(nbytes):
            o = st["o"]
            st["o"] += (nbytes + 63) // 64 * 64
            assert st["o"] <= ARENA_W * 4, "SBUF overflow %d" % st["o"]
            return o
        return a

    def vf32(off, n):
        return arena[:, off // 4: off // 4 + n]

    def vbf(off, n):
        return arena[:, off // 4: off // 4 + (n + 1) // 2].bitcast(BF16)[:, 0:n]

    alloc = region(0)
    ident = vf32(alloc(512), 128)
    ones = vbf(alloc(256), 128)
    pvec = vf32(alloc(PV_N * 4), PV_N)
    etab = vf32(alloc(28 * 128 * 4), 28 * 128)
    slots = [vbf(alloc(SLOT_ELEMS * 2), SLOT_ELEMS) for _ in range(NSLOT)]
    b_slot = [Buf("slot%d" % i) for i in range(NSLOT)]
    ds_slot = [new_dsem() for _ in range(NSLOT)]
    PH0 = alloc(0)

    b_const = Buf("const")
    ds_c = new_dsem()
    psum = [nc.alloc_psum_tensor("ps%d" % i, [128, 512], F32)[:] for i in range(8)]
    b_ps = [Buf("ps%d" % i, excl=True) for i in range(8)]

    block = es.enter_context(nc.Block())

    S.op("pool", "dma_start", out=ident, in_=ident_d, writes=[b_const], dsem=ds_c)
    S.op("pool", "dma_start", out=pvec, in_=pvec_d, writes=[b_const], dsem=ds_c)
    S.op("pool", "dma_start", out=etab, in_=etab_d, writes=[b_const], dsem=ds_c)
    b_ones = Buf("ones")
    S.op("dve", "memset", ones, 1.0, writes=[b_ones])

    conv_buf = {}

    def conv_ffn(p, part=0):
        w1, w3, w2 = p + "w1", p + "w3", p + "w2"
        for j in (range(11) if part in (0, 1) else ()):
            sl = slice(j * 512, (j + 1) * 512)
            b = Buf("cv")
            ds = new_dsem()
            S.op("pool", "dma_start", out=wb[w1][:, sl], in_=w_d[w1][:, sl], writes=[b], dsem=ds)
            S.op("pool", "dma_start", out=wb[w3][:, sl], in_=w_d[w3][:, sl], writes=[b], dsem=ds)
            conv_buf[(w1, j)] = b
            conv_buf[(w3, j)] = b
        for g in (range(4) if part in (0, 2) else ()):
            b = Buf("cv")
            ds = new_dsem()
            for dc in range(4 * g, 4 * g + 4):
                S.op("pool", "dma_start",
                     out=wb[w2][dc].rearrange("p (fc j) -> p fc j", j=128),
                     in_=w_d[w2][:, dc * 128:(dc + 1) * 128].rearrange("(fc p) j -> p fc j", p=128),
                     writes=[b], dsem=ds)
                conv_buf[(w2, dc)] = b

    def conv_cols(key, ncols, per):
        nt = ncols // 512
        for g in range(0, nt, per):
            hi = min(nt, g + per)
            sl = slice(g * 512, hi * 512)
            b = Buf("cv_" + key)
            ds = new_dsem()
            S.op("pool", "dma_start", out=wb[key][:, sl], in_=w_d[key][:, sl], writes=[b], dsem=ds)
            for j in range(g, hi):
                conv_buf[(key, j)] = b

    deferred_conv = []
    conv_ready = [False]

    def conv_p2():
        conv_cols("win", DIN, 3)
        conv_cols("wgate", 2 * D, 2)
        conv_cols("wco", D, 4)
        conv_cols("wao", D, 4)
        conv_cols("wo", D, 2)

    if "P1" in phases:
        deferred_conv.append(lambda: conv_ffn("f1"))
        if "P2" in phases:
            deferred_conv.append(conv_p2)
        if "P3" in phases:
            deferred_conv.append(lambda: conv_ffn("f2", 1))
            deferred_conv.append(lambda: conv_ffn("f2", 2))
    else:
        if "P2" in phases:
            conv_p2()
        if "P3" in phases:
            conv_ffn("f2")

    stream = []
    wstate = {"next_load": 0, "next_use": 0}

    def colblk(name, c0, ncols, kc, e0):
        src = wb[name][:, c0:c0 + ncols].rearrange("(c p) n -> p c n", p=128)
        return (e0, kc, ncols, src, conv_buf[(name, c0 // 512)])

    def tile_parts(key):
        kind, j = key
        if kind in ("f1s1", "f2s1"):
            p = kind[:2]
            return [colblk(p + "w1", 256 * j, 256, 16, 0), colblk(p + "w3", 256 * j, 256, 16, 4096)]
        if kind in ("f1w2", "f2w2"):
            return [(0, None, NFC * 128, wb[kind][j], conv_buf[(kind, j)])]
        if kind == "win":
            return [colblk("win", 512 * j, 512, 16, 0)]
        if kind == "ccx":
            return [colblk("win", 1024 + 256 * j, 256, 16, 0), colblk("win", 2048 + 256 * j, 256, 16, 4096)]
        if kind == "gc":
            return [colblk("wgate", 256 * j, 256, 16, 0), colblk("wco", 256 * j, 256, 8, 4096)]
        if kind == "ga":
            return [colblk("wgate", 2048 + 256 * j, 256, 16, 0), colblk("wao", 256 * j, 256, 4, 4096)]
        if kind == "wo":
            return [colblk("wo", 512 * j, 512, 16, 0)]
        raise KeyError(key)

    def emit_load(i):
        s = i % NSLOT
        for (e0, kc, ncols, src, cb) in tile_parts(stream[i]):
            if kc is None:
                dst = slots[s][:, e0:e0 + ncols]
            else:
                dst = slots[s][:, e0:e0 + kc * ncols].rearrange("p (c n) -> p c n", c=kc)
            S.op("sp", "dma_start", out=dst, in_=src, reads=[cb], writes=[b_slot[s]], dsem=ds_slot[s])

    def wget(key):
        i = wstate["next_use"]
        assert stream[i] == key, (i, stream[i], key)
        while wstate["next_load"] < min(len(stream), i + NSLOT):
            emit_load(wstate["next_load"])
            wstate["next_load"] += 1
        wstate["next_use"] = i + 1
        s = i % NSLOT
        return slots[s], b_slot[s]

    def ffn_stream(p):
        return [(p + "s1", j) for j in range(22)] + [(p + "w2", dc) for dc in range(NCH)]

    P2A_STREAM = [("win", j) for j in range(9, 15)]
    P2B_STREAM = [("win", j) for j in range(6, 9)]
    P2C_STREAM = ([("ccx", j) for j in range(4)] + [("win", 0), ("win", 1)] + [("gc", j) for j in range(8)]
                  + [("ga", j) for j in range(8)] + [("wo", g) for g in range(4)])

    for s in range(nseq):
        if "P1" in phases:
            stream += ffn_stream("f1") * ntiles
        if "P2" in phases:
            stream += P2A_STREAM * ntiles + P2B_STREAM * ntiles + P2C_STREAM * ntiles
        if "P3" in phases:
            stream += ffn_stream("f2") * ntiles

    rr = {}

    def nxt(k, n=2):
        v = rr.get(k, 0)
        rr[k] = (v + 1) % n
        return v

    def gcol(base, c):
        return pvec[:, base + c: base + c + 1]

    dram_tiles = {}

    def dbuf(name, s, ti):
        k = (name, s, ti)
        if k not in dram_tiles:
            dram_tiles[k] = Buf("%s_%d_%d" % k)
        return dram_tiles[k]

    ds_st = [new_dsem() for _ in range(4)]

    def store(dst_ap, src_ap, src_bufs, dkey):
        i = nxt("st", 4)
        S.op("pool", "dma_start", out=dst_ap, in_=src_ap, reads=list(src_bufs), writes=[dbuf(*dkey)], dsem=ds_st[i])

    fa = region(PH0)
    o_xT = fa(NCH * 2048)
    xT = [vf32(o_xT + c * 2048, 512) for c in range(NCH)]
    xT_all = vf32(o_xT, NCH * 512).rearrange("p (c n) -> p c n", c=NCH)
    o_hid = fa(NFC * 1024)
    hid = [vbf(o_hid + f * 1024, 512) for f in range(NFC)]
    ytok = [vf32(o_hid + i * 8192, D) for i in range(2)]
    xblk = [vf32(o_hid + b * 8192, D) for b in range(4)]
    o_stg = fa(NCH * 2048)
    stg = [vf32(o_stg + c * 2048, 512) for c in range(NCH)]
    o_xn = fa(NCH * 1024)
    xnT = [vbf(o_xn + c * 1024, 512) for c in range(NCH)]
    sq = [vbf(fa(1024), 512) for i in range(2)]
    rstds = [vf32(fa(2048), 512) for i in range(2)]
    tmp = [vf32(fa(2048), 512) for i in range(2)]
    tt = [vf32(fa(2048), 512) for i in range(2)]
    ub = [vbf(fa(1024), 512) for i in range(2)]

    b_xT = [Buf("xT%d" % c) for c in range(NCH)]
    b_hid = [Buf("hid%d" % f) for f in range(NFC)]
    b_stg = [Buf("stg%d" % c) for c in range(NCH)]
    b_xn = [Buf("xn%d" % c) for c in range(NCH)]
    b_sq = [Buf("sq0"), Buf("sq1")]
    b_rstds = [Buf("rstd0"), Buf("rstd1")]
    b_tmp = [Buf("tmp0"), Buf("tmp1")]
    b_tt = [Buf("tt0"), Buf("tt1")]
    b_ub = [Buf("ub0"), Buf("ub1")]
    ds_x = [new_dsem() for _ in range(4)]

    def stat_chunk(src_ap, src_buf, ss, first, last, eng, sqv=None, bsq=None, key="sq"):
        sqv = sqv or sq
        bsq = bsq or b_sq
        i = nxt(key)
        S.op("act", "activation", sqv[i], src_ap, AF.Square, reads=[src_buf], writes=[bsq[i]])
        S.op("pe", "matmul", psum[ss], lhsT=ones, rhs=sqv[i], start=first, stop=last, reads=[bsq[i], b_ones], writes=[b_ps[ss]])

    def finish_rstd(ss, rv, brv, half=False):
        S.op("act", "activation", rv, psum[ss], AF.Sqrt, bias=EPS, scale=1.0 / D, reads=[b_ps[ss]], writes=[brv])
        S.op("dve", "reciprocal", rv, rv, reads=[brv], writes=[brv])
        if half:
            S.op("dve", "tensor_scalar_mul", rv, rv, 0.5, reads=[brv], writes=[brv])

    def new_rstd():
        i = nxt("rstd")
        return rstds[i], b_rstds[i]

    def ffn_core(p, ss_bank=6):
        w2 = p + "w2"
        cnt = 0
        for j in range(22):
            sl_, bsl_ = wget((p + "s1", j))
            v1 = sl_[:, 0:4096].rearrange("p (c n) -> p c n", c=16)
            v3 = sl_[:, 4096:8192].rearrange("p (c n) -> p c n", c=16)
            for m in range(2):
                fc = 2 * j + m
                pa = 0 + 2 * (cnt % 2)
                pc = 1 + 2 * (cnt % 2)
                cnt += 1
                for kc in range(NCH):
                    S.op("pe", "matmul", psum[pa], lhsT=v1[:, kc, m * 128:(m + 1) * 128], rhs=xnT[kc], start=(kc == 0), stop=(kc == NCH - 1),
                         reads=[bsl_, b_xn[kc]], writes=[b_ps[pa]])
                for kc in range(NCH):
                    S.op("pe", "matmul", psum[pc], lhsT=v3[:, kc, m * 128:(m + 1) * 128], rhs=xnT[kc], start=(kc == 0), stop=(kc == NCH - 1),
                         reads=[bsl_, b_xn[kc]], writes=[b_ps[pc]])
                i = nxt("tmp")
                S.op("act", "activation", tmp[i], psum[pa], AF.Silu, reads=[b_ps[pa]], writes=[b_tmp[i]])
                S.op("dve", "tensor_tensor", hid[fc], tmp[i], psum[pc], ALU.mult, reads=[b_tmp[i], b_ps[pc]], writes=[b_hid[fc]])
        for dc in range(NCH):
            s2, bs2 = wget((w2, dc))
            v2 = s2[:, 0:NFC * 128].rearrange("p (f j) -> p f j", j=128)
            py = 4 + (dc % 2)
            for fc in range(NFC):
                S.op("pe", "matmul", psum[py], lhsT=v2[:, fc, :], rhs=hid[fc], start=(fc == 0), stop=(fc == NFC - 1),
                     reads=[bs2, b_hid[fc]], writes=[b_ps[py]])
            S.op("dve", "tensor_copy", stg[dc], psum[py], reads=[b_ps[py]], writes=[b_stg[dc]])
            stat_chunk(psum[py], b_ps[py], ss_bank, dc == 0, dc == NCH - 1, "act")
        rv, brv = new_rstd()
        finish_rstd(ss_bank, rv, brv, half=True)
        return rv, brv

    def post_residual(c, rv, brv, gbase):
        i = nxt("tt")
        S.op("dve", "scalar_tensor_tensor", tt[i], stg[c], gcol(gbase, c), rv, ALU.mult, ALU.mult,
             reads=[b_stg[c], brv, b_const], writes=[b_tt[i]])
        S.op("pool", "tensor_tensor", stg[c], tt[i], xT[c], ALU.add, reads=[b_tt[i], b_xT[c]], writes=[b_stg[c]])

    def p1_load(s, ti):
        tok0 = s * SEQ + ti * T
        for b in range(4):
            S.op(XQ, "dma_start", out=xblk[b], in_=x_d[tok0 + b * 128: tok0 + (b + 1) * 128, :],
                 reads=([conv_buf[("f1w2", 15)]] if _os.environ.get("MK_E4") else []),
                 writes=b_hid[8 * b:8 * b + 8], dsem=ds_x[b])

    def norm_to_xn(gbase, ss_bank):
        rv, brv = new_rstd()
        finish_rstd(ss_bank, rv, brv)
        for c in range(NCH):
            S.op("dve", "scalar_tensor_tensor", xnT[c], xT[c], gcol(gbase, c), rv, ALU.mult, ALU.mult,
                 reads=[b_xT[c], brv, b_const], writes=[b_xn[c]])

    def phase_p1(s, ti, hook):
        tok0 = s * SEQ + ti * T
        for g in range(4):
            for b in range(4):
                pt = 4 + ((g * 4 + b) % 2)
                for cc in range(4):
                    c = 4 * g + cc
                    S.op("pe", "transpose", psum[pt][:, cc * 128:(cc + 1) * 128], xblk[b][:, c * 128:(c + 1) * 128], ident,
                         reads=b_hid[8 * b:8 * b + 8] + [b_const], writes=[b_ps[pt]])
                S.op("dve", "tensor_copy", xT_all[:, 4 * g:4 * g + 4, b * 128:(b + 1) * 128], psum[pt].rearrange("p (c n) -> p c n", c=4),
                     reads=[b_ps[pt]], writes=b_xT[4 * g:4 * g + 4])
        for c in range(NCH):
            stat_chunk(xT[c], b_xT[c], 6, c == 0, c == NCH - 1, "act")
        norm_to_xn(PV_PRE1, 6)
        if conv_ready[0] and deferred_conv:
            deferred_conv.pop(0)()
        rv, brv = ffn_core("f1")
        hook()
        for c in range(NCH):
            post_residual(c, rv, brv, PV_POST1)
            store(Hs[c, :, tok0:tok0 + T], stg[c], [b_stg[c]], ("Hs", s, ti))
            stat_chunk(stg[c], b_stg[c], 7, c == 0, c == NCH - 1, "act")
        rv2, brv2 = new_rstd()
        finish_rstd(7, rv2, brv2)
        for c in range(NCH):
            i = nxt("ub")
            S.op("dve", "scalar_tensor_tensor", ub[i], stg[c], gcol(PV_MPRE, c), rv2, ALU.mult, ALU.mult,
                 reads=[b_stg[c], brv2, b_const], writes=[b_ub[i]])
            store(Us[c, :, tok0:tok0 + T], ub[i], [b_ub[i]], ("Us", s, ti))

    ds_h2 = [new_dsem() for _ in range(4)]

    def p3_load(s, ti):
        tok0 = s * SEQ + ti * T
        for g in range(4):
            S.op("sp", "dma_start", out=xT_all[:, 4 * g:4 * g + 4, :], in_=H2s[4 * g:4 * g + 4, :, tok0:tok0 + T].rearrange("c p t -> p c t"),
                 reads=[dbuf("H2s", s, ti)], writes=b_xT[4 * g:4 * g + 4], dsem=ds_h2[g])

    def phase_p3(s, ti, hook):
        tok0 = s * SEQ + ti * T
        for c in range(NCH):
            stat_chunk(xT[c], b_xT[c], 6, c == 0, c == NCH - 1, "act")
        norm_to_xn(PV_PRE2, 6)
        rv, brv = ffn_core("f2")
        for c in range(NCH):
            post_residual(c, rv, brv, PV_POST2)
        hook()
        for b in range(4):
            yb = b % 2
            ybufs = b_hid[yb * 8: yb * 8 + 8]
            for g in range(4):
                pt = 4 + ((b * 4 + g) % 2)
                for cc in range(4):
                    c = 4 * g + cc
                    S.op("pe", "transpose", psum[pt][:, cc * 128:(cc + 1) * 128], stg[c][:, b * 128:(b + 1) * 128], ident,
                         reads=[b_stg[c], b_const], writes=[b_ps[pt]])
                if nxt("ev") == 0:
                    S.op("act", "copy", ytok[yb][:, g * 512:(g + 1) * 512], psum[pt], reads=[b_ps[pt]], writes=ybufs[2 * g:2 * g + 2])
                else:
                    S.op("dve", "tensor_copy", ytok[yb][:, g * 512:(g + 1) * 512], psum[pt], reads=[b_ps[pt]], writes=ybufs[2 * g:2 * g + 2])
            store(y_d[tok0 + b * 128: tok0 + (b + 1) * 128, :], ytok[yb], ybufs, ("y", s, ti))

    pa_ = region(PH0)
    o_KT = pa_(12 * SEQ * 2)
    KT = [[vbf(o_KT + (g * 4 + h) * SEQ * 2, SEQ) for h in range(4)] for g in range(3)]
    b_KT = [[Buf("KT%d%d" % (g, h)) for h in range(4)] for g in range(3)]
    o_V = pa_(3 * 16 * 512 * 2)
    Vg = [vbf(o_V + g * 16 * 512 * 2, 16 * 512) for g in range(3)]
    b_V = [Buf("V%d" % g) for g in range(3)]
    o_uT = pa_(NCH * 1024)
    uTs = [[vbf(o + c * 1024, 512) for c in range(NCH)] for o in (o_uT, o_V)]
    uTs_all = [vbf(o, NCH * 512).rearrange("p (c n) -> p c n", c=NCH) for o in (o_uT, o_V)]
    b_uTs = [Buf("uT"), b_V[0]]
    o_vq = pa_(4 * DQKV * 2)
    vst = [vbf(o_vq + b * DQKV * 2, DQKV) for b in range(4)]
    b_vst = [Buf("vst%d" % b) for b in range(4)]
    Qt = [[vbf(o_vq + (g * 4 + h) * 1024, 512) for h in range(4)] for g in range(3)]
    b_Q = [[b_vst[(g * 4 + h) // 3] for h in range(4)] for g in range(3)]
    ptmp = [vf32(pa_(2048), 512) for i in range(2)]
    b_ptmp = [Buf("ptmp0"), Buf("ptmp1")]
    pbf = [vbf(pa_(1024), 512) for i in range(2)]
    b_pbf = [Buf("pbf0"), Buf("pbf1")]
    rl = vf32(pa_(2048), 512)
    b_rl = Buf("rl")
    ob16 = [vbf(pa_(1024), 512) for i in range(2)]
    b_ob16 = [Buf("ob0"), Buf("ob1")]
    ds_u = new_dsem()
    ds_u2 = [new_dsem(), new_dsem()]
    ds_v = new_dsem()

    def load_uT(s, ti, dst_all, dst_buf, ds=None):
        tok0 = s * SEQ + ti * T
        S.op("sp", "dma_start", out=dst_all, in_=Us[:, :, tok0:tok0 + T].rearrange("c p t -> p c t"),
             reads=[dbuf("Us", s, ti)], writes=[dst_buf], dsem=(ds or ds_u))

    def p2a_load(s, ti):
        load_uT(s, ti, uTs_all[ti % 2], b_uTs[ti % 2], ds_u2[ti % 2])

    def p2b_load(s, ti):
        load_uT(s, ti, uTs_all[0], b_uTs[0], ds_u2[0])

    def proj_chunk(wv, bsl, m, bank, uTl, buT):
        for kc in range(NCH):
            S.op("pe", "matmul", psum[bank], lhsT=wv[:, kc, m * 128:(m + 1) * 128], rhs=uTl[kc], start=(kc == 0), stop=(kc == NCH - 1),
                 reads=[bsl, buT], writes=[b_ps[bank]])

    def evac_perm(src_bank, g, h, ti, eng, is_q):
        src = psum[src_bank]
        if is_q:
            dst = Qt[g][h]
            if g == 0:
                o_ap, i_ap = dst, src
            else:
                r = 4 if g == 1 else 16
                o_ap, i_ap = dst.rearrange("p (r j) -> p r j", r=r), src.rearrange("p (j r) -> p r j", r=r)
            wbufs = [b_Q[g][h]]
        else:
            dst = KT[g][h]
            if g == 0:
                o_ap, i_ap = dst[:, ti * T:(ti + 1) * T], src
            else:
                r = 4 if g == 1 else 16
                w = T // r
                o_ap = dst.rearrange("p (r l) -> p r l", r=r)[:, :, ti * w:(ti + 1) * w]
                i_ap = src.rearrange("p (j r) -> p r j", r=r)
            wbufs = [b_KT[g][h]]
        if eng == "act":
            S.op("act", "copy", o_ap, i_ap, reads=[b_ps[src_bank]], writes=wbufs)
        else:
            S.op("dve", "tensor_copy", o_ap, i_ap, reads=[b_ps[src_bank]], writes=wbufs)

    def phase_p2a(s, ti, hook):
        tok0 = s * SEQ + ti * T
        uT, b_uT = uTs[ti % 2], b_uTs[ti % 2]
        hook()
        cnt = 0
        for g in range(3):
            sl, bsl = wget(("win", 9 + g))
            wv = sl.rearrange("p (c n) -> p c n", c=16)
            for hh in range(4):
                bank = cnt % 2
                cnt += 1
                proj_chunk(wv, bsl, hh, bank, uT, b_uT)
                evac_perm(bank, g, hh, ti, "act" if cnt % 2 else "dve", False)
        for g in range(3):
            sl, bsl = wget(("win", 12 + g))
            wv = sl.rearrange("p (c n) -> p c n", c=16)
            for b in range(4):
                bank = cnt % 2
                cnt += 1
                for kc in range(NCH):
                    S.op("pe", "matmul", psum[bank], lhsT=uT[kc][:, b * 128:(b + 1) * 128], rhs=wv[:, kc, :], start=(kc == 0), stop=(kc == NCH - 1),
                         reads=[bsl, b_uT], writes=[b_ps[bank]])
                if cnt % 2:
                    S.op("act", "copy", vst[b][:, g * 512:(g + 1) * 512], psum[bank], reads=[b_ps[bank]], writes=[b_vst[b]])
                else:
                    S.op("dve", "tensor_copy", vst[b][:, g * 512:(g + 1) * 512], psum[bank], reads=[b_ps[bank]], writes=[b_vst[b]])
        for b in range(4):
            store(Vs[tok0 + b * 128: tok0 + (b + 1) * 128, :], vst[b], [b_vst[b]], ("Vs", s, ti))

    def load_V(s):
        s0 = s * SEQ
        rd = [dbuf("Vs", s, ti) for ti in range(ntiles)]
        S.op("pool", "dma_start", out=Vg[0].rearrange("p (c n) -> p c n", c=16),
             in_=Vs[s0:s0 + SEQ, 0:512].rearrange("(c p) n -> p c n", p=128), reads=rd, writes=[b_V[0]], dsem=ds_v)
        V1d = Vg[1].rearrange("p (c r n) -> p c r n", c=4, r=4)
        for c in range(4):
            S.op("pool", "dma_start", out=V1d[:, c, :, :],
                 in_=Vs[s0 + c * 512:s0 + (c + 1) * 512, 512:1024].rearrange("(p r) n -> p r n", r=4), reads=rd, writes=[b_V[1]], dsem=ds_v)
        S.op("pool", "dma_start", out=Vg[2].rearrange("p (r n) -> p r n", r=16),
             in_=Vs[s0:s0 + SEQ, 1024:1536].rearrange("(p r) n -> p r n", r=16), reads=rd, writes=[b_V[2]], dsem=ds_v)

    SCALE = float(128 ** -0.5)

    def etile(g, h):
        if g < 2:
            base = ((g * 4 + h) * 3) * 128
            return etab[:, base: base + 384]
        base = (24 + h) * 128
        return etab[:, base: base + 128]

    def phase_p2b(s, ti, hook):
        tok0 = s * SEQ + ti * T
        uT, b_uT = uTs[0], b_uTs[0]
        cnt = 0
        for g in range(3):
            sl, bsl = wget(("win", 6 + g))
            wv = sl.rearrange("p (c n) -> p c n", c=16)
            for hh in range(4):
                bank = cnt % 2
                cnt += 1
                proj_chunk(wv, bsl, hh, bank, uT, b_uT)
                evac_perm(bank, g, hh, ti, "act" if cnt % 2 else "dve", True)
        hook()
        nblk = SEQ // 128
        V0 = Vg[0].rearrange("p (c n) -> p c n", c=16)
        V1 = Vg[1].rearrange("p (c r n) -> p c r n", c=4, r=4)
        V2 = Vg[2].rearrange("p (r n) -> p r n", r=16)
        for hh in range(4):
            ob = 4 + 2 * (hh % 2)
            lb = 5 + 2 * (hh % 2)
            first = [True]

            def pv(o_ap, l_ap, v_ap, p_ap, pbuf, vbuf, ob=ob, lb=lb, first=first):
                st = first[0]
                first[0] = False
                S.op("pe", "matmul", o_ap, lhsT=v_ap, rhs=p_ap, start=st, stop=False, skip_group_check=True,
                     reads=[pbuf, vbuf], writes=[b_ps[ob]])
                S.op("pe", "matmul", l_ap, lhsT=ones, rhs=p_ap, start=st, stop=False, skip_group_check=True,
                     reads=[pbuf, b_ones], writes=[b_ps[lb]])

            def softmax_part(sb, lo, hi, e_ap):
                i = nxt("pt")
                S.op("act", "activation", ptmp[i][:, lo:hi], psum[sb][:, lo:hi], AF.Exp, scale=SCALE, reads=[b_ps[sb]], writes=[b_ptmp[i]])
                S.op("dve", "tensor_tensor", pbf[i][:, lo:hi], ptmp[i][:, lo:hi], e_ap, ALU.mult, reads=[b_ptmp[i], b_const], writes=[b_pbf[i]])
                return i

            for qb in range(4):
                B = 4 * ti + qb
                rels = [r for r in range(3) if 0 <= B - 1 + r < nblk]
                sb = 2 + (cnt % 2)
                cnt += 1
                for r in rels:
                    kc = B - 1 + r
                    S.op("pe", "matmul", psum[sb][:, r * 128:(r + 1) * 128], lhsT=KT[0][hh][:, kc * 128:(kc + 1) * 128],
                         rhs=Qt[0][hh][:, qb * 128:(qb + 1) * 128], start=True, stop=True,
                         reads=[b_KT[0][hh], b_Q[0][hh]], writes=[b_ps[sb]])
                lo, hi = rels[0] * 128, (rels[-1] + 1) * 128
                i = softmax_part(sb, lo, hi, etile(0, hh)[:, lo:hi])
                for r in rels:
                    kc = B - 1 + r
                    pv(psum[ob][:, qb * 128:(qb + 1) * 128], psum[lb][:, qb * 128:(qb + 1) * 128],
                       V0[:, kc, hh * 128:(hh + 1) * 128], pbf[i][:, r * 128:(r + 1) * 128], b_pbf[i], b_V[0])
            K1 = KT[1][hh].rearrange("p (r l) -> p r l", r=4)
            Q1 = Qt[1][hh].rearrange("p (r j) -> p r j", r=4)
            O1 = psum[ob].rearrange("p (j r) -> p r j", r=4)
            L1 = psum[lb].rearrange("p (j r) -> p r j", r=4)
            for rc in range(4):
                rels = [r for r in range(3) if 0 <= ti - 1 + r < 4]
                sb = 2 + (cnt % 2)
                cnt += 1
                for r in rels:
                    kc = ti - 1 + r
                    S.op("pe", "matmul", psum[sb][:, r * 128:(r + 1) * 128], lhsT=K1[:, rc, kc * 128:(kc + 1) * 128], rhs=Q1[:, rc, :],
                         start=True, stop=True, reads=[b_KT[1][hh], b_Q[1][hh]], writes=[b_ps[sb]])
                lo, hi = rels[0] * 128, (rels[-1] + 1) * 128
                i = softmax_part(sb, lo, hi, etile(1, hh)[:, lo:hi])
                for r in rels:
                    kc = ti - 1 + r
                    pv(O1[:, rc, :], L1[:, rc, :], V1[:, kc, rc, hh * 128:(hh + 1) * 128], pbf[i][:, r * 128:(r + 1) * 128], b_pbf[i], b_V[1])
            K2 = KT[2][hh].rearrange("p (r l) -> p r l", r=16)
            Q2 = Qt[2][hh].rearrange("p (r j) -> p r j", r=16)
            O2 = psum[ob].rearrange("p (j r) -> p r j", r=16)
            L2 = psum[lb].rearrange("p (j r) -> p r j", r=16)
            sb = 2 + (cnt % 2)
            cnt += 1
            for rc in range(16):
                S.op("pe", "matmul", psum[sb][:, rc * 32:(rc + 1) * 32], lhsT=K2[:, rc, :], rhs=Q2[:, rc, :], start=True, stop=True,
                     reads=[b_KT[2][hh], b_Q[2][hh]], writes=[b_ps[sb]])
            i = nxt("pt")
            S.op("act", "activation", ptmp[i], psum[sb], AF.Exp, scale=SCALE, reads=[b_ps[sb]], writes=[b_ptmp[i]])
            e2 = etile(2, hh)[:, ti * 32:(ti + 1) * 32]
            S.op("dve", "tensor_tensor", pbf[i].rearrange("p (r j) -> p r j", r=16), ptmp[i].rearrange("p (r j) -> p r j", r=16),
                 e2.unsqueeze(1).broadcast_to([128, 16, 32]), ALU.mult, reads=[b_ptmp[i], b_const], writes=[b_pbf[i]])
            for rc in range(16):
                pv(O2[:, rc, :], L2[:, rc, :], V2[:, rc, hh * 128:(hh + 1) * 128], pbf[i][:, rc * 32:(rc + 1) * 32], b_pbf[i], b_V[2])
            S.op("dve", "reciprocal", rl, psum[lb], reads=[b_ps[lb]], writes=[b_rl])
            oi = nxt("ob")
            S.op("dve", "tensor_tensor", ob16[oi], psum[ob], rl, ALU.mult, reads=[b_ps[ob], b_rl], writes=[b_ob16[oi]])
            store(Os[hh, :, tok0:tok0 + T], ob16[oi], [b_ob16[oi]], ("Os", s, ti))

    wc = region(PH0)
    o_uT2 = [wc(NCH * 1024) for i in range(2)]
    uT2s = [[vbf(o + c * 1024, 512) for c in range(NCH)] for o in o_uT2]
    uT2s_all = [vbf(o, NCH * 512).rearrange("p (c n) -> p c n", c=NCH) for o in o_uT2]
    b_uT2s = [Buf("uT2a"), Buf("uT2b")]
    uhs = [vbf(wc(NCH * 32 * 2), NCH * 32).rearrange("p (c n) -> p c n", c=NCH) for i in range(2)]
    b_uhs = [Buf("uh0"), Buf("uh1")]
    ds_uh = [new_dsem(), new_dsem()]
    ds_uc = [new_dsem(), new_dsem()]
    o_ot = wc(4 * 1024)
    otile = [vbf(o_ot + h * 1024, 512) for h in range(4)]
    otile_all = vbf(o_ot, 4 * 512).rearrange("p (c n) -> p c n", c=4)
    b_ot = Buf("otile")
    ds_ot = new_dsem()
    zf = [vf32(wc(516 * 4), 514) for i in range(2)]
    b_zf = [Buf("zf0"), Buf("zf1")]
    tcc = [vf32(wc(2048), 512) for i in range(2)]
    b_tcc = [Buf("tcc0"), Buf("tcc1")]
    th = vf32(wc(64), 4)
    b_th = Buf("th")
    cacc = [vf32(wc(2048), 512) for i in range(8)]
    b_cacc = [Buf("cacc%d" % i) for i in range(8)]
    o_yc = wc(8 * 1024)
    ycin = [vbf(o_yc + i * 1024, 512) for i in range(8)]
    b_yc = [Buf("yc%d" % i) for i in range(8)]
    mg = [vbf(wc(1024), 512) for c in range(NCH)]
    b_mg = [Buf("mg%d" % c) for c in range(NCH)]
    gt = [vf32(wc(2048), 512) for i in range(4)]
    b_gt = [Buf("gt%d" % i) for i in range(4)]
    stg2 = [vf32(wc(2048), 512) for c in range(NCH)]
    b_stg2 = [Buf("stgm%d" % c) for c in range(NCH)]
    ds_hr = [new_dsem() for i in range(8)]
    hres = cacc + [vf32(o_yc + j * 2048, 512) for j in range(4)] + gt
    hres_bufs = [[b_cacc[j]] for j in range(8)] + [[b_yc[2 * j], b_yc[2 * j + 1]] for j in range(4)] + [[b_gt[j]] for j in range(4)]
    sq2 = [vbf(wc(1024), 512) for i in range(2)]
    b_sq2 = [Buf("sqm0"), Buf("sqm1")]
    rstd2 = vf32(wc(2048), 512)
    b_rstd2 = Buf("rstdm")
    tt2 = [vf32(wc(2048), 512) for i in range(2)]
    b_tt2 = [Buf("ttm0"), Buf("ttm1")]

    def p2c_load(s, ti):
        tok0 = s * SEQ + ti * T
        k = ti % 2
        load_uT(s, ti, uT2s_all[k], b_uT2s[k], ds_uc[k])
        S.op("dve", "memset", uhs[k], 0.0, writes=[b_uhs[k]])
        if ti > 0:
            S.op("sp", "dma_start", out=uhs[k][:, :, 0:16], in_=Us[:, :, tok0 - 16:tok0].rearrange("c p t -> p c t"),
                 reads=[dbuf("Us", s, ti - 1)], writes=[b_uhs[k]], dsem=ds_uh[k])
        if ti < NT - 1:
            S.op("sp", "dma_start", out=uhs[k][:, :, 16:32], in_=Us[:, :, tok0 + T:tok0 + T + 16].rearrange("c p t -> p c t"),
                 reads=[dbuf("Us", s, ti + 1)], writes=[b_uhs[k]], dsem=ds_uh[k])

    def phase_p2c(s, ti, hook):
        tok0 = s * SEQ + ti * T
        k = ti % 2
        uT2, b_uT2, uh, b_uh = uT2s[k], b_uT2s[k], uhs[k], b_uhs[k]
        S.op("sp", "dma_start", out=otile_all, in_=Os[:, :, tok0:tok0 + T].rearrange("c p t -> p c t"),
             reads=[dbuf("Os", s, ti)], writes=[b_ot], dsem=ds_ot)
        hook()
        HL = 6
        for q in range(4):
            sx, bsx = wget(("ccx", q))
            vcc = sx[:, 0:4096].rearrange("p (c n) -> p c n", c=16)
            vcx = sx[:, 4096:8192].rearrange("p (c n) -> p c n", c=16)
            for m in range(2):
                i8 = 2 * q + m
                bcc_k = 0 + 2 * (i8 % 2)
                bcx_k = 1 + 2 * (i8 % 2)
                for (wv, bank, which) in ((vcc, bcc_k, 0), (vcx, bcx_k, 1)):
                    hcol = (i8 * 2 + which) * 2
                    for kc in range(NCH):
                        S.op("pe", "matmul", psum[bank], lhsT=wv[:, kc, m * 128:(m + 1) * 128], rhs=uT2[kc], start=(kc == 0), stop=(kc == NCH - 1),
                             reads=[bsx, b_uT2], writes=[b_ps[bank]])
                        S.op("pe", "matmul", psum[HL][:, hcol:hcol + 2], lhsT=wv[:, kc, m * 128:(m + 1) * 128], rhs=uh[:, kc, 15:17],
                             start=(kc == 0), stop=(kc == NCH - 1), reads=[bsx, b_uh], writes=[b_ps[HL]])
                zi = nxt("zf")
                ci = nxt("tcc")
                hc = (i8 * 2) * 2
                S.op("act", "copy", tcc[ci], psum[bcc_k], reads=[b_ps[bcc_k]], writes=[b_tcc[ci]])
                S.op("dve", "tensor_tensor", zf[zi][:, 1:513], tcc[ci], psum[bcx_k], ALU.mult, reads=[b_tcc[ci], b_ps[bcx_k]], writes=[b_zf[zi]])
                S.op("act", "copy", th[:, 0:2], psum[HL][:, hc:hc + 2], reads=[b_ps[HL]], writes=[b_th])
                S.op("dve", "tensor_tensor", zf[zi][:, 0:514:513], th[:, 0:2], psum[HL][:, hc + 2:hc + 4], ALU.mult,
                     reads=[b_th, b_ps[HL]], writes=[b_zf[zi]])
                S.op("dve", "tensor_scalar_mul", cacc[i8], zf[zi][:, 0:512], gcol(PV_CW0, i8), reads=[b_zf[zi], b_const], writes=[b_cacc[i8]])
                S.op("dve", "scalar_tensor_tensor", cacc[i8], zf[zi][:, 1:513], gcol(PV_CW1, i8), cacc[i8], ALU.mult, ALU.add,
                     reads=[b_zf[zi], b_cacc[i8], b_const], writes=[b_cacc[i8]])
                S.op("dve", "scalar_tensor_tensor", cacc[i8], zf[zi][:, 2:514], gcol(PV_CW2, i8), cacc[i8], ALU.mult, ALU.add,
                     reads=[b_zf[zi], b_cacc[i8], b_const], writes=[b_cacc[i8]])
        for hf in range(2):
            scb, bcb = wget(("win", hf))
            vcb = scb.rearrange("p (c n) -> p c n", c=16)
            for m in range(4):
                i8 = hf * 4 + m
                bank = 4 + (m % 2)
                proj_chunk(vcb, bcb, m, bank, uT2, b_uT2)
                S.op("dve", "scalar_tensor_tensor", ycin[i8], cacc[i8], gcol(PV_CB, i8), psum[bank], ALU.add, ALU.mult,
                     reads=[b_cacc[i8], b_ps[bank], b_const], writes=[b_yc[i8]])
        for q in range(8):
            sg, bsg = wget(("gc", q))
            vg = sg[:, 0:4096].rearrange("p (c n) -> p c n", c=16)
            vco = sg[:, 4096:4096 + 2048].rearrange("p (c n) -> p c n", c=8)
            for m in range(2):
                dch = 2 * q + m
                A = 0 + 2 * (dch % 2)
                C = 1 + 2 * (dch % 2)
                msl = slice(m * 128, (m + 1) * 128)
                for kc in range(8):
                    S.op("pe", "matmul", psum[A], lhsT=vco[:, kc, msl], rhs=ycin[kc], start=(kc == 0), stop=(kc == 7),
                         reads=[bsg, b_yc[kc]], writes=[b_ps[A]])
                proj_chunk(vg, bsg, m, C, uT2, b_uT2)
                i0 = nxt("gt", 4)
                S.op("act", "activation", gt[i0], psum[C], AF.Sigmoid, bias=gcol(PV_BG, dch), scale=1.0, reads=[b_ps[C], b_const], writes=[b_gt[i0]])
                S.op("dve", "tensor_tensor", stg2[dch], gt[i0], psum[A], ALU.mult, reads=[b_gt[i0], b_ps[A]], writes=[b_stg2[dch]])
        for q in range(8):
            sg, bsg = wget(("ga", q))
            vg = sg[:, 0:4096].rearrange("p (c n) -> p c n", c=16)
            vao = sg[:, 4096:4096 + 1024].rearrange("p (c n) -> p c n", c=4)
            for m in range(2):
                dch = 2 * q + m
                Bk = 4 + 2 * (dch % 2)
                Dk = 5 + 2 * (dch % 2)
                msl = slice(m * 128, (m + 1) * 128)
                for hh in range(4):
                    S.op("pe", "matmul", psum[Bk], lhsT=vao[:, hh, msl], rhs=otile[hh], start=(hh == 0), stop=(hh == 3),
                         reads=[bsg, b_ot], writes=[b_ps[Bk]])
                proj_chunk(vg, bsg, m, Dk, uT2, b_uT2)
                i1 = nxt("gt", 4)
                S.op("act", "activation", gt[i1], psum[Dk], AF.Sigmoid, bias=gcol(PV_BG, 16 + dch), scale=1.0, reads=[b_ps[Dk], b_const], writes=[b_gt[i1]])
                S.op("dve", "tensor_tensor", gt[i1], gt[i1], psum[Bk], ALU.mult, reads=[b_gt[i1], b_ps[Bk]], writes=[b_gt[i1]])
                S.op("pool", "tensor_tensor", mg[dch], stg2[dch], gt[i1], ALU.add, reads=[b_stg2[dch], b_gt[i1]], writes=[b_mg[dch]])
        for c in range(NCH):
            S.op("sp", "dma_start", out=hres[c], in_=Hs[c, :, tok0:tok0 + T], reads=[dbuf("Hs", s, ti)], writes=hres_bufs[c], dsem=ds_hr[c % 8])
        for g in range(4):
            swo, bwo = wget(("wo", g))
            vwo = swo.rearrange("p (c n) -> p c n", c=16)
            for m in range(4):
                d2 = 4 * g + m
                py = d2 % 2
                for kc in range(NCH):
                    S.op("pe", "matmul", psum[py], lhsT=vwo[:, kc, m * 128:(m + 1) * 128], rhs=mg[kc], start=(kc == 0), stop=(kc == NCH - 1),
                         reads=[bwo, b_mg[kc]], writes=[b_ps[py]])
                S.op("dve", "tensor_copy", stg2[d2], psum[py], reads=[b_ps[py]], writes=[b_stg2[d2]])
                stat_chunk(psum[py], b_ps[py], 2, d2 == 0, d2 == NCH - 1, "act", sq2, b_sq2, "sq2")
        finish_rstd(2, rstd2, b_rstd2)
        for c in range(NCH):
            i = nxt("tt2")
            S.op("dve", "scalar_tensor_tensor", tt2[i], stg2[c], gcol(PV_MPOST, c), rstd2, ALU.mult, ALU.mult,
                 reads=[b_stg2[c], b_rstd2, b_const], writes=[b_tt2[i]])
            S.op("pool", "tensor_tensor", stg2[c], tt2[i], hres[c], ALU.add, reads=[b_tt2[i]] + hres_bufs[c], writes=[b_stg2[c]])
            store(H2s[c, :, tok0:tok0 + T], stg2[c], [b_stg2[c]], ("H2s", s, ti))

    def run_group(items, preloaded=False, chain=()):
        if not items:
            return
        if not preloaded:
            items[0][1](items[0][2], items[0][3])
        allit = list(items) + list(chain)
        for k, (fn, ld, s_, ti_) in enumerate(items):
            def hook(k=k):
                if k + 1 < len(allit):
                    nfn, nld, ns, nti = allit[k + 1]
                    nld(ns, nti)
            fn(s_, ti_, hook)

    first_p1 = True
    for s in range(nseq):
        if "P1" in phases:
            items = [(phase_p1, p1_load, s, ti) for ti in range(ntiles)]
            if first_p1:
                p1_load(s, 0)
                deferred_conv.pop(0)()
                if ntiles < 4:
                    for f in deferred_conv:
                        f()
                    deferred_conv.clear()
                run_group(items[:1], preloaded=True, chain=items[1:2])
                conv_ready[0] = True
                run_group(items[1:], preloaded=True)
                for f in deferred_conv:
                    f()
                deferred_conv.clear()
            else:
                run_group(items)
            first_p1 = False
        if "P2" in phases:
            S.barrier()
            run_group([(phase_p2a, p2a_load, s, ti) for ti in range(ntiles)])
            load_V(s)
            run_group([(phase_p2b, p2b_load, s, ti) for ti in range(ntiles)])
            S.barrier()
            run_group([(phase_p2c, p2c_load, s, ti) for ti in range(ntiles)])
            S.barrier()
        if "P3" in phases:
            run_group([(phase_p3, p3_load, s, ti) for ti in range(ntiles)])
    assert wstate["next_use"] == len(stream), (wstate, len(stream))
    S.barrier()
    S.emit(block, sems)
    import os
    if os.environ.get("MK_STATS"):
        print("MK_STATS ops/signals", S.stats, "max dsem", max(d.count for d in S.dsems), "ndsem", len(S.dsems), flush=True)
    es.close()
    return nc


def make_consts(inp):
    def col(v):
        v = np.asarray(v, np.float32).reshape(-1)
        return v.reshape(-1, 128).T
    pv = np.zeros((128, PV_N), np.float32)
    pv[:, PV_PRE1:PV_PRE1 + 16] = col(inp["ffn1_norm_pre"])
    pv[:, PV_POST1:PV_POST1 + 16] = col(inp["ffn1_norm_post"])
    pv[:, PV_MPRE:PV_MPRE + 16] = col(inp["mix_norm_pre"])
    pv[:, PV_MPOST:PV_MPOST + 16] = col(inp["mix_norm_post"])
    pv[:, PV_PRE2:PV_PRE2 + 16] = col(inp["ffn2_norm_pre"])
    pv[:, PV_POST2:PV_POST2 + 16] = col(inp["ffn2_norm_post"])
    cw = np.asarray(inp["conv_w"], np.float32).reshape(3, DCONV)
    pv[:, PV_CW0:PV_CW0 + 8] = col(cw[0])
    pv[:, PV_CW1:PV_CW1 + 8] = col(cw[1])
    pv[:, PV_CW2:PV_CW2 + 8] = col(cw[2])
    pv[:, PV_CB:PV_CB + 8] = col(inp["conv_b"])
    pv[:, PV_BG:PV_BG + 32] = col(inp["b_gate"])
    ident = np.eye(128, dtype=np.float32)
    i = np.arange(1, 13, dtype=np.float32)
    slopes = np.exp2(np.float32(-8.0) * i / np.float32(12)).astype(np.float32).reshape(3, 4)
    dil = (1, 4, 16)
    kk = np.arange(128)[:, None]
    qq = np.arange(128)[None, :]
    et = np.zeros((128, 28 * 128), np.float32)
    for g in range(2):
        for h in range(4):
            for rel in range(3):
                delta = np.abs((rel - 1) * 128 + kk - qq)
                e = np.exp(-(slopes[g, h] * np.float32(dil[g]) * delta.astype(np.float32))).astype(np.float32)
                e = np.where(delta <= 64, e, np.float32(0))
                base = ((g * 4 + h) * 3 + rel) * 128
                et[:, base:base + 128] = e
    for h in range(4):
        delta = np.abs(kk - qq)
        e = np.exp(-(slopes[2, h] * np.float32(16) * delta.astype(np.float32))).astype(np.float32)
        e = np.where(delta <= 64, e, np.float32(0))
        et[:, (24 + h) * 128:(25 + h) * 128] = e
    return pv, ident, et


_NC_CACHE = {}


def kernel(x_prompt, x_sample, ffn1_norm_pre, ffn1_w1, ffn1_w3, ffn1_w2, ffn1_norm_post,
           mix_norm_pre, w_in, conv_w, conv_b, w_conv_out, w_attn_out, w_gate, b_gate, w_o,
           mix_norm_post, ffn2_norm_pre, ffn2_w1, ffn2_w3, ffn2_w2, ffn2_norm_post):
    inp = dict(ffn1_norm_pre=ffn1_norm_pre, ffn1_norm_post=ffn1_norm_post, mix_norm_pre=mix_norm_pre,
               mix_norm_post=mix_norm_post, ffn2_norm_pre=ffn2_norm_pre, ffn2_norm_post=ffn2_norm_post,
               conv_w=conv_w, conv_b=conv_b, b_gate=b_gate)
    pv, ident, et = make_consts(inp)
    xp = np.asarray(x_prompt, np.float32)
    xs = np.asarray(x_sample, np.float32)
    nb_p, nb_s = xp.shape[0], xs.shape[0]
    seqs = [("p", i) for i in range(nb_p)] + [("s", i) for i in range(nb_s)]
    nreal = len(seqs)
    assert nreal <= NCORES * NSEQ_CORE
    slot_of = {}
    order = []
    for k in range(NSEQ_CORE):
        for c in range(NCORES):
            order.append(c * NSEQ_CORE + k)
    for n, sq_ in enumerate(seqs):
        slot_of[order[n]] = sq_

    def get(sq_):
        return xp[sq_[1]] if sq_[0] == "p" else xs[sq_[1]]

    w2d = dict(
        f1w1=np.asarray(ffn1_w1, np.float32)[0], f1w3=np.asarray(ffn1_w3, np.float32)[0], f1w2=np.asarray(ffn1_w2, np.float32)[0],
        win=np.asarray(w_in, np.float32)[0], wgate=np.asarray(w_gate, np.float32)[0], wco=np.asarray(w_conv_out, np.float32)[0],
        wao=np.asarray(w_attn_out, np.float32)[0], wo=np.asarray(w_o, np.float32)[0],
        f2w1=np.asarray(ffn2_w1, np.float32)[0], f2w3=np.asarray(ffn2_w3, np.float32)[0], f2w2=np.asarray(ffn2_w2, np.float32)[0],
    )
    in_maps = []
    for c in range(NCORES):
        xs_c = []
        for k in range(NSEQ_CORE):
            sl = c * NSEQ_CORE + k
            xs_c.append(get(slot_of.get(sl, seqs[0])))
        m = dict(w2d)
        m["x"] = np.ascontiguousarray(np.concatenate(xs_c, axis=0))
        m["pvec"] = pv
        m["ident"] = ident
        m["etab"] = et
        in_maps.append(m)
    if "nc" not in _NC_CACHE:
        _NC_CACHE["nc"] = build()
    res = run_bass_kernel_spmd(_NC_CACHE["nc"], in_maps, core_ids=list(range(NCORES)))
    yp = np.empty_like(xp)
    ys = np.empty_like(xs)
    for sl, sq_ in slot_of.items():
        c, k = divmod(sl, NSEQ_CORE)
        blk = res.results[c]["y"][k * SEQ:(k + 1) * SEQ]
        if sq_[0] == "p":
            yp[sq_[1]] = blk
        else:
            ys[sq_[1]] = blk
    return (yp, ys)
```

```python
import contextlib
import numpy as np
import concourse.bass as bass
import concourse.mybir as mybir
from concourse.bass_utils import run_bass_kernel_spmd

F32 = mybir.dt.float32
BF16 = mybir.dt.bfloat16
ALU = mybir.AluOpType
AF = mybir.ActivationFunctionType

D = 2048
DFF = 5632
SEQ = 2048
T = 512
NT = SEQ // T
NCH = D // 128
NFC = DFF // 128
DCONV = 1024
DQKV = 1536
DIN = 7680
EPS = 1e-6
NCORES = 8
NSEQ_CORE = 3
SLOT_ELEMS = 16 * 512
import os as _os
XQ = _os.environ.get("MK_XQ", "pool")
NSLOT = 3
PF = 2

PV_PRE1, PV_POST1, PV_MPRE, PV_MPOST, PV_PRE2, PV_POST2 = 0, 16, 32, 48, 64, 80
PV_CW0, PV_CW1, PV_CW2, PV_CB, PV_BG = 96, 104, 112, 120, 128
PV_N = 160
ENGS = ("pe", "act", "dve", "pool", "sp")


class Buf:
    __slots__ = ("name", "last_w", "readers", "excl")

    def __init__(self, name, excl=False):
        self.name = name
        self.last_w = None
        self.readers = []
        self.excl = excl


class DSem:
    __slots__ = ("sem", "count", "last_op")

    def __init__(self, sem):
        self.sem = sem
        self.count = 0
        self.last_op = None


class Op:
    __slots__ = ("eng", "meth", "args", "kw", "deps", "is_dma", "dsem", "dval", "signal", "val")

    def __init__(self, eng, meth, args, kw, is_dma=False, dsem=None):
        self.eng = eng
        self.meth = meth
        self.args = args
        self.kw = kw
        self.deps = []
        self.is_dma = is_dma
        self.dsem = dsem
        self.dval = 0
        self.signal = False
        self.val = 0


class Sched:
    def __init__(self):
        self.q = {e: [] for e in ENGS}
        self.dsems = []

    def _dep(self, op, d, kind):
        if d is None or d is op:
            return
        if (not d.is_dma) and (not op.is_dma) and d.eng == op.eng:
            if kind == "raw" and op.eng != "pe":
                op.deps.append(d)
            return
        op.deps.append(d)

    def op(self, eng, meth, *args, reads=(), writes=(), dsem=None, **kw):
        dma = dsem is not None
        o = Op(eng, meth, args, kw, is_dma=dma, dsem=dsem)
        if any(b.excl for b in reads):
            writes = list(writes) + [b for b in reads if b.excl and b not in writes]
            reads = [b for b in reads if not b.excl]
        for b in reads:
            self._dep(o, b.last_w, "raw")
        for b in writes:
            self._dep(o, b.last_w, "waw")
            for r in b.readers:
                self._dep(o, r, "war")
        for b in reads:
            if not dma:
                b.readers = [r for r in b.readers if r.is_dma or r.eng != eng]
            b.readers.append(o)
        for b in writes:
            b.last_w = o
            b.readers = []
        if dma:
            if dsem.last_op is not None:
                o.deps.append(dsem.last_op)
            dsem.count += 16
            o.dval = dsem.count
            dsem.last_op = o
        self.q[eng].append(o)
        return o

    def barrier(self):
        drains = {}
        for e in ENGS:
            o = Op(e, "drain", (), {})
            self.q[e].append(o)
            drains[e] = o
        for e in ENGS:
            o2 = Op(e, "nop", (), {})
            o2.deps = [drains[f] for f in ENGS if f != e] + [ds.last_op for ds in self.dsems if ds.last_op is not None]
            self.q[e].append(o2)

    def emit(self, block, sems):
        for e in self.q:
            for o in self.q[e]:
                for d in o.deps:
                    if not d.is_dma:
                        d.signal = True
        for e in self.q:
            c = 0
            for o in self.q[e]:
                if not o.is_dma and o.signal:
                    c += 1
                    o.val = c
            self.stats = getattr(self, "stats", {})
            self.stats[e] = (len(self.q[e]), c)

        def run(engname, handle):
            seen = {}
            for o in self.q[engname]:
                for d in o.deps:
                    if d.is_dma:
                        s, v = d.dsem.sem, d.dval
                    else:
                        s, v = sems[d.eng], d.val
                    k = id(s)
                    if seen.get(k, 0) >= v:
                        continue
                    seen[k] = v
                    handle.wait_ge(s, v)
                ins = getattr(handle, o.meth)(*o.args, **o.kw)
                if o.is_dma:
                    ins.then_inc(o.dsem.sem, 16)
                elif o.signal:
                    ins.then_inc(sems[engname], 1)

        @block.tensor
        def _(h):
            run("pe", h)

        @block.scalar
        def _(h):
            run("act", h)

        @block.vector
        def _(h):
            run("dve", h)

        @block.gpsimd
        def _(h):
            run("pool", h)

        @block.sync
        def _(h):
            run("sp", h)


def build(nseq=NSEQ_CORE, phases=("P1", "P2", "P3"), ntiles=NT, debug=False):
    NTOK = nseq * SEQ
    nc = bass.Bass("TRN2", target_bir_lowering=False)
    es = contextlib.ExitStack()

    def din(name, shape, dt=F32):
        return nc.dram_tensor(name, list(shape), dt, kind="ExternalInput").ap()

    def dscr(name, shape, dt, out=False):
        return nc.dram_tensor(name, list(shape), dt, kind=("ExternalOutput" if out else "Internal")).ap()

    x_d = din("x", [NTOK, D])
    w_d = {}
    if "P1" in phases:
        w_d.update({"f1w1": din("f1w1", [D, DFF]), "f1w3": din("f1w3", [D, DFF]), "f1w2": din("f1w2", [DFF, D])})
    if "P2" in phases:
        w_d.update({"win": din("win", [D, DIN]), "wgate": din("wgate", [D, 2 * D]),
                    "wco": din("wco", [DCONV, D]), "wao": din("wao", [512, D]), "wo": din("wo", [D, D])})
    if "P3" in phases:
        w_d.update({"f2w1": din("f2w1", [D, DFF]), "f2w3": din("f2w3", [D, DFF]), "f2w2": din("f2w2", [DFF, D])})
    pvec_d = din("pvec", [128, PV_N])
    ident_d = din("ident", [128, 128])
    etab_d = din("etab", [128, 28 * 128])
    y_d = nc.dram_tensor("y", [NTOK, D], F32, kind="ExternalOutput").ap()

    wb = {}
    for k in w_d:
        if k.endswith("w2"):
            wb[k] = dscr(k + "b", [NCH, 128, NFC * 128], BF16)
        else:
            wb[k] = dscr(k + "b", list(w_d[k].shape), BF16)
    Hs = dscr("Hs", [NCH, 128, NTOK], F32, out=debug)
    Us = dscr("Us", [NCH, 128, NTOK], BF16, out=debug)
    H2s = dscr("H2s", [NCH, 128, NTOK], F32, out=debug)
    Os = dscr("Os", [4, 128, NTOK], BF16, out=debug)
    Vs = dscr("Vs", [NTOK, DQKV], BF16, out=debug)

    S = Sched()
    sems = {e: es.enter_context(nc.semaphore("s_" + e)) for e in ENGS}

    def new_dsem():
        ds = DSem(es.enter_context(nc.semaphore("d%d" % len(S.dsems))))
        S.dsems.append(ds)
        return ds

    ARENA_W = 205 * 1024 // 4
    arena = nc.alloc_sbuf_tensor("arena", [128, ARENA_W], F32)

    def region(start):
        st = {"o": start}

        def a(nbytes):
            o = st["o"]
            st["o"] += (nbytes + 63) // 64 * 64
            assert st["o"] <= ARENA_W * 4, "SBUF overflow %d" % st["o"]
            return o
        return a

    def vf32(off, n):
        return arena[:, off // 4: off // 4 + n]

    def vbf(off, n):
        return arena[:, off // 4: off // 4 + (n + 1) // 2].bitcast(BF16)[:, 0:n]

    alloc = region(0)
    ident = vf32(alloc(512), 128)
    ones = vbf(alloc(256), 128)
    pvec = vf32(alloc(PV_N * 4), PV_N)
    etab = vf32(alloc(28 * 128 * 4), 28 * 128)
    slots = [vbf(alloc(SLOT_ELEMS * 2), SLOT_ELEMS) for _ in range(NSLOT)]
    b_slot = [Buf("slot%d" % i) for i in range(NSLOT)]
    ds_slot = [new_dsem() for _ in range(NSLOT)]
    PH0 = alloc(0)

    b_const = Buf("const")
    ds_c = new_dsem()
    psum = [nc.alloc_psum_tensor("ps%d" % i, [128, 512], F32)[:] for i in range(8)]
    b_ps = [Buf("ps%d" % i, excl=True) for i in range(8)]

    block = es.enter_context(nc.Block())

    S.op("pool", "dma_start", out=ident, in_=ident_d, writes=[b_const], dsem=ds_c)
    S.op("pool", "dma_start", out=pvec, in_=pvec_d, writes=[b_const], dsem=ds_c)
    S.op("pool", "dma_start", out=etab, in_=etab_d, writes=[b_const], dsem=ds_c)
    b_ones = Buf("ones")
    S.op("dve", "memset", ones, 1.0, writes=[b_ones])

    conv_buf = {}

    def conv_ffn(p, part=0):
        w1, w3, w2 = p + "w1", p + "w3", p + "w2"
        for j in (range(11) if part in (0, 1) else ()):
            sl = slice(j * 512, (j + 1) * 512)
            b = Buf("cv")
            ds = new_dsem()
            S.op("pool", "dma_start", out=wb[w1][:, sl], in_=w_d[w1][:, sl], writes=[b], dsem=ds)
            S.op("pool", "dma_start", out=wb[w3][:, sl], in_=w_d[w3][:, sl], writes=[b], dsem=ds)
            conv_buf[(w1, j)] = b
            conv_buf[(w3, j)] = b
        for g in (range(4) if part in (0, 2) else ()):
            b = Buf("cv")
            ds = new_dsem()
            for dc in range(4 * g, 4 * g + 4):
                S.op("pool", "dma_start",
                     out=wb[w2][dc].rearrange("p (fc j) -> p fc j", j=128),
                     in_=w_d[w2][:, dc * 128:(dc + 1) * 128].rearrange("(fc p) j -> p fc j", p=128),
                     writes=[b], dsem=ds)
                conv_buf[(w2, dc)] = b

    def conv_cols(key, ncols, per):
        nt = ncols // 512
        for g in range(0, nt, per):
            hi = min(nt, g + per)
            sl = slice(g * 512, hi * 512)
            b = Buf("cv_" + key)
            ds = new_dsem()
            S.op("pool", "dma_start", out=wb[key][:, sl], in_=w_d[key][:, sl], writes=[b], dsem=ds)
            for j in range(g, hi):
                conv_buf[(key, j)] = b

    deferred_conv = []
    conv_ready = [False]

    def conv_p2():
        conv_cols("win", DIN, 3)
        conv_cols("wgate", 2 * D, 2)
        conv_cols("wco", D, 4)
        conv_cols("wao", D, 4)
        conv_cols("wo", D, 2)

    if "P1" in phases:
        deferred_conv.append(lambda: conv_ffn("f1"))
        if "P2" in phases:
            deferred_conv.append(conv_p2)
        if "P3" in phases:
            deferred_conv.append(lambda: conv_ffn("f2", 1))
            deferred_conv.append(lambda: conv_ffn("f2", 2))
    else:
        if "P2" in phases:
            conv_p2()
        if "P3" in phases:
            conv_ffn("f2")

    stream = []
    wstate = {"next_load": 0, "next_use": 0}

    def colblk(name, c0, ncols, kc, e0):
        src = wb[name][:, c0:c0 + ncols].rearrange("(c p) n -> p c n", p=128)
        return (e0, kc, ncols, src, conv_buf[(name, c0 // 512)])

    def tile_parts(key):
        kind, j = key
        if kind in ("f1s1", "f2s1"):
            p = kind[:2]
            return [colblk(p + "w1", 256 * j, 256, 16, 0), colblk(p + "w3", 256 * j, 256, 16, 4096)]
        if kind in ("f1w2", "f2w2"):
            return [(0, None, NFC * 128, wb[kind][j], conv_buf[(kind, j)])]
        if kind == "win":
            return [colblk("win", 512 * j, 512, 16, 0)]
        if kind == "ccx":
            return [colblk("win", 1024 + 256 * j, 256, 16, 0), colblk("win", 2048 + 256 * j, 256, 16, 4096)]
        if kind == "gc":
            return [colblk("wgate", 256 * j, 256, 16, 0), colblk("wco", 256 * j, 256, 8, 4096)]
        if kind == "ga":
            return [colblk("wgate", 2048 + 256 * j, 256, 16, 0), colblk("wao", 256 * j, 256, 4, 4096)]
        if kind == "wo":
            return [colblk("wo", 512 * j, 512, 16, 0)]
        raise KeyError(key)

    def emit_load(i):
        s = i % NSLOT
        for (e0, kc, ncols, src, cb) in tile_parts(stream[i]):
            if kc is None:
                dst = slots[s][:, e0:e0 + ncols]
            else:
                dst = slots[s][:, e0:e0 + kc * ncols].rearrange("p (c n) -> p c n", c=kc)
            S.op("sp", "dma_start", out=dst, in_=src, reads=[cb], writes=[b_slot[s]], dsem=ds_slot[s])

    def wget(key):
        i = wstate["next_use"]
        assert stream[i] == key, (i, stream[i], key)
        while wstate["next_load"] < min(len(stream), i + NSLOT):
            emit_load(wstate["next_load"])
            wstate["next_load"] += 1
        wstate["next_use"] = i + 1
        s = i % NSLOT
        return slots[s], b_slot[s]

    def ffn_stream(p):
        return [(p + "s1", j) for j in range(22)] + [(p + "w2", dc) for dc in range(NCH)]

    P2A_STREAM = [("win", j) for j in range(9, 15)]
    P2B_STREAM = [("win", j) for j in range(6, 9)]
    P2C_STREAM = ([("ccx", j) for j in range(4)] + [("win", 0), ("win", 1)] + [("gc", j) for j in range(8)]
                  + [("ga", j) for j in range(8)] + [("wo", g) for g in range(4)])

    for s in range(nseq):
        if "P1" in phases:
            stream += ffn_stream("f1") * ntiles
        if "P2" in phases:
            stream += P2A_STREAM * ntiles + P2B_STREAM * ntiles + P2C_STREAM * ntiles
        if "P3" in phases:
            stream += ffn_stream("f2") * ntiles

    rr = {}

    def nxt(k, n=2):
        v = rr.get(k, 0)
        rr[k] = (v + 1) % n
        return v

    def gcol(base, c):
        return pvec[:, base + c: base + c + 1]

    dram_tiles = {}

    def dbuf(name, s, ti):
        k = (name, s, ti)
        if k not in dram_tiles:
            dram_tiles[k] = Buf("%s_%d_%d" % k)
        return dram_tiles[k]

    ds_st = [new_dsem() for _ in range(4)]

    def store(dst_ap, src_ap, src_bufs, dkey):
        i = nxt("st", 4)
        S.op("pool", "dma_start", out=dst_ap, in_=src_ap, reads=list(src_bufs), writes=[dbuf(*dkey)], dsem=ds_st[i])

    fa = region(PH0)
    o_xT = fa(NCH * 2048)
    xT = [vf32(o_xT + c * 2048, 512) for c in range(NCH)]
    xT_all = vf32(o_xT, NCH * 512).rearrange("p (c n) -> p c n", c=NCH)
    o_hid = fa(NFC * 1024)
    hid = [vbf(o_hid + f * 1024, 512) for f in range(NFC)]
    ytok = [vf32(o_hid + i * 8192, D) for i in range(2)]
    xblk = [vf32(o_hid + b * 8192, D) for b in range(4)]
    o_stg = fa(NCH * 2048)
    stg = [vf32(o_stg + c * 2048, 512) for c in range(NCH)]
    o_xn = fa(NCH * 1024)
    xnT = [vbf(o_xn + c * 1024, 512) for c in range(NCH)]
    sq = [vbf(fa(1024), 512) for i in range(2)]
    rstds = [vf32(fa(2048), 512) for i in range(2)]
    tmp = [vf32(fa(2048), 512) for i in range(2)]
    tt = [vf32(fa(2048), 512) for i in range(2)]
    ub = [vbf(fa(1024), 512) for i in range(2)]

    b_xT = [Buf("xT%d" % c) for c in range(NCH)]
    b_hid = [Buf("hid%d" % f) for f in range(NFC)]
    b_stg = [Buf("stg%d" % c) for c in range(NCH)]
    b_xn = [Buf("xn%d" % c) for c in range(NCH)]
    b_sq = [Buf("sq0"), Buf("sq1")]
    b_rstds = [Buf("rstd0"), Buf("rstd1")]
    b_tmp = [Buf("tmp0"), Buf("tmp1")]
    b_tt = [Buf("tt0"), Buf("tt1")]
    b_ub = [Buf("ub0"), Buf("ub1")]
    ds_x = [new_dsem() for _ in range(4)]

    def stat_sq(src_ap, src_buf, sqv=None, bsq=None, key="sq"):
        sqv = sqv or sq
        bsq = bsq or b_sq
        i = nxt(key)
        S.op("act", "activation", sqv[i], src_ap, AF.Square, reads=[src_buf], writes=[bsq[i]])
        return i

    def stat_mm(i, ss, first, last, sqv=None, bsq=None):
        sqv = sqv or sq
        bsq = bsq or b_sq
        S.op("pe", "matmul", psum[ss], lhsT=ones, rhs=sqv[i], start=first, stop=last, reads=[bsq[i], b_ones], writes=[b_ps[ss]])

    def stat_chunk(src_ap, src_buf, ss, first, last, eng, sqv=None, bsq=None, key="sq"):
        stat_mm(stat_sq(src_ap, src_buf, sqv, bsq, key), ss, first, last, sqv, bsq)

    def finish_rstd(ss, rv, brv, half=False):
        S.op("act", "activation", rv, psum[ss], AF.Sqrt, bias=EPS, scale=1.0 / D, reads=[b_ps[ss]], writes=[brv])
        S.op("dve", "reciprocal", rv, rv, reads=[brv], writes=[brv])
        if half:
            S.op("dve", "tensor_scalar_mul", rv, rv, 0.5, reads=[brv], writes=[brv])

    def new_rstd():
        i = nxt("rstd")
        return rstds[i], b_rstds[i]

    def ffn_core(p, ss_bank=6):
        w2 = p + "w2"
        cnt = 0
        for j in range(22):
            sl_, bsl_ = wget((p + "s1", j))
            v1 = sl_[:, 0:4096].rearrange("p (c n) -> p c n", c=16)
            v3 = sl_[:, 4096:8192].rearrange("p (c n) -> p c n", c=16)
            for m in range(2):
                fc = 2 * j + m
                pa = 0 + 2 * (cnt % 2)
                pc = 1 + 2 * (cnt % 2)
                cnt += 1
                for kc in range(NCH):
                    S.op("pe", "matmul", psum[pa], lhsT=v1[:, kc, m * 128:(m + 1) * 128], rhs=xnT[kc], start=(kc == 0), stop=(kc == NCH - 1),
                         reads=[bsl_, b_xn[kc]], writes=[b_ps[pa]])
                for kc in range(NCH):
                    S.op("pe", "matmul", psum[pc], lhsT=v3[:, kc, m * 128:(m + 1) * 128], rhs=xnT[kc], start=(kc == 0), stop=(kc == NCH - 1),
                         reads=[bsl_, b_xn[kc]], writes=[b_ps[pc]])
                i = nxt("tmp")
                S.op("act", "activation", tmp[i], psum[pa], AF.Silu, reads=[b_ps[pa]], writes=[b_tmp[i]])
                S.op("dve", "tensor_tensor", hid[fc], tmp[i], psum[pc], ALU.mult, reads=[b_tmp[i], b_ps[pc]], writes=[b_hid[fc]])
        pend = None
        for dc in range(NCH):
            s2, bs2 = wget((w2, dc))
            v2 = s2[:, 0:NFC * 128].rearrange("p (f j) -> p f j", j=128)
            py = 4 + (dc % 2)
            for fc in range(NFC):
                S.op("pe", "matmul", psum[py], lhsT=v2[:, fc, :], rhs=hid[fc], start=(fc == 0), stop=(fc == NFC - 1),
                     reads=[bs2, b_hid[fc]], writes=[b_ps[py]])
            S.op("dve", "tensor_copy", stg[dc], psum[py], reads=[b_ps[py]], writes=[b_stg[dc]])
            si = stat_sq(psum[py], b_ps[py])
            if pend is not None:
                stat_mm(pend[0], ss_bank, pend[1] == 0, False)
            pend = (si, dc)
        stat_mm(pend[0], ss_bank, False, True)
        rv, brv = new_rstd()
        finish_rstd(ss_bank, rv, brv, half=True)
        return rv, brv

    def post_residual(c, rv, brv, gbase):
        i = nxt("tt")
        S.op("dve", "scalar_tensor_tensor", tt[i], stg[c], gcol(gbase, c), rv, ALU.mult, ALU.mult,
             reads=[b_stg[c], brv, b_const], writes=[b_tt[i]])
        S.op("pool" if c % 2 == 0 else "dve", "tensor_tensor", stg[c], tt[i], xT[c], ALU.add, reads=[b_tt[i], b_xT[c]], writes=[b_stg[c]])

    def p1_load(s, ti):
        tok0 = s * SEQ + ti * T
        for b in range(4):
            S.op(XQ, "dma_start", out=xblk[b], in_=x_d[tok0 + b * 128: tok0 + (b + 1) * 128, :],
                 reads=([conv_buf[("f1w2", 15)]] if _os.environ.get("MK_E4") else []),
                 writes=b_hid[8 * b:8 * b + 8], dsem=ds_x[b])

    def norm_to_xn(gbase, ss_bank):
        rv, brv = new_rstd()
        finish_rstd(ss_bank, rv, brv)
        for c in range(NCH):
            S.op("dve", "scalar_tensor_tensor", xnT[c], xT[c], gcol(gbase, c), rv, ALU.mult, ALU.mult,
                 reads=[b_xT[c], brv, b_const], writes=[b_xn[c]])

    def phase_p1(s, ti, hook):
        tok0 = s * SEQ + ti * T
        for g in range(4):
            for b in range(4):
                pt = 4 + ((g * 4 + b) % 2)
                for cc in range(4):
                    c = 4 * g + cc
                    S.op("pe", "transpose", psum[pt][:, cc * 128:(cc + 1) * 128], xblk[b][:, c * 128:(c + 1) * 128], ident,
                         reads=b_hid[8 * b:8 * b + 8] + [b_const], writes=[b_ps[pt]])
                S.op("dve", "tensor_copy", xT_all[:, 4 * g:4 * g + 4, b * 128:(b + 1) * 128], psum[pt].rearrange("p (c n) -> p c n", c=4),
                     reads=[b_ps[pt]], writes=b_xT[4 * g:4 * g + 4])
        for c in range(NCH):
            stat_chunk(xT[c], b_xT[c], 6, c == 0, c == NCH - 1, "act")
        norm_to_xn(PV_PRE1, 6)
        if conv_ready[0] and deferred_conv:
            deferred_conv.pop(0)()
        rv, brv = ffn_core("f1")
        hook()
        for c in range(NCH):
            post_residual(c, rv, brv, PV_POST1)
            store(Hs[c, :, tok0:tok0 + T], stg[c], [b_stg[c]], ("Hs", s, ti))
            stat_chunk(stg[c], b_stg[c], 7, c == 0, c == NCH - 1, "act")
        rv2, brv2 = new_rstd()
        finish_rstd(7, rv2, brv2)
        for c in range(NCH):
            i = nxt("ub")
            S.op("dve", "scalar_tensor_tensor", ub[i], stg[c], gcol(PV_MPRE, c), rv2, ALU.mult, ALU.mult,
                 reads=[b_stg[c], brv2, b_const], writes=[b_ub[i]])
            store(Us[c, :, tok0:tok0 + T], ub[i], [b_ub[i]], ("Us", s, ti))

    ds_h2 = [new_dsem() for _ in range(4)]

    def p3_load(s, ti):
        tok0 = s * SEQ + ti * T
        for g in range(4):
            S.op("sp", "dma_start", out=xT_all[:, 4 * g:4 * g + 4, :], in_=H2s[4 * g:4 * g + 4, :, tok0:tok0 + T].rearrange("c p t -> p c t"),
                 reads=[dbuf("H2s", s, ti)], writes=b_xT[4 * g:4 * g + 4], dsem=ds_h2[g])

    def phase_p3(s, ti, hook):
        tok0 = s * SEQ + ti * T
        for c in range(NCH):
            stat_chunk(xT[c], b_xT[c], 6, c == 0, c == NCH - 1, "act")
        norm_to_xn(PV_PRE2, 6)
        rv, brv = ffn_core("f2")
        for c in range(NCH):
            post_residual(c, rv, brv, PV_POST2)
        hook()
        for b in range(4):
            yb = b % 2
            ybufs = b_hid[yb * 8: yb * 8 + 8]
            for g in range(4):
                pt = 4 + ((b * 4 + g) % 2)
                for cc in range(4):
                    c = 4 * g + cc
                    S.op("pe", "transpose", psum[pt][:, cc * 128:(cc + 1) * 128], stg[c][:, b * 128:(b + 1) * 128], ident,
                         reads=[b_stg[c], b_const], writes=[b_ps[pt]])
                if nxt("ev") == 0:
                    S.op("act", "copy", ytok[yb][:, g * 512:(g + 1) * 512], psum[pt], reads=[b_ps[pt]], writes=ybufs[2 * g:2 * g + 2])
                else:
                    S.op("dve", "tensor_copy", ytok[yb][:, g * 512:(g + 1) * 512], psum[pt], reads=[b_ps[pt]], writes=ybufs[2 * g:2 * g + 2])
            store(y_d[tok0 + b * 128: tok0 + (b + 1) * 128, :], ytok[yb], ybufs, ("y", s, ti))

    pa_ = region(PH0)
    o_KT = pa_(12 * SEQ * 2)
    KT = [[vbf(o_KT + (g * 4 + h) * SEQ * 2, SEQ) for h in range(4)] for g in range(3)]
    b_KT = [[Buf("KT%d%d" % (g, h)) for h in range(4)] for g in range(3)]
    o_V = pa_(3 * 16 * 512 * 2)
    Vg = [vbf(o_V + g * 16 * 512 * 2, 16 * 512) for g in range(3)]
    b_V = [Buf("V%d" % g) for g in range(3)]
    o_uT = pa_(NCH * 1024)
    uTs = [[vbf(o + c * 1024, 512) for c in range(NCH)] for o in (o_uT, o_V)]
    uTs_all = [vbf(o, NCH * 512).rearrange("p (c n) -> p c n", c=NCH) for o in (o_uT, o_V)]
    b_uTs = [Buf("uT"), b_V[0]]
    o_vq = pa_(4 * DQKV * 2)
    vst = [vbf(o_vq + b * DQKV * 2, DQKV) for b in range(4)]
    b_vst = [Buf("vst%d" % b) for b in range(4)]
    Qt = [[vbf(o_vq + (g * 4 + h) * 1024, 512) for h in range(4)] for g in range(3)]
    b_Q = [[b_vst[(g * 4 + h) // 3] for h in range(4)] for g in range(3)]
    ptmp = [vf32(pa_(2048), 512) for i in range(4)]
    b_ptmp = [Buf("ptmp%d" % i) for i in range(4)]
    pbf = [vbf(pa_(1024), 512) for i in range(4)]
    b_pbf = [Buf("pbf%d" % i) for i in range(4)]
    rl = vf32(pa_(2048), 512)
    b_rl = Buf("rl")
    ob16 = [vbf(pa_(1024), 512) for i in range(2)]
    b_ob16 = [Buf("ob0"), Buf("ob1")]
    ds_u = new_dsem()
    ds_u2 = [new_dsem(), new_dsem()]
    ds_v = new_dsem()

    def load_uT(s, ti, dst_all, dst_buf, ds=None):
        tok0 = s * SEQ + ti * T
        S.op("sp", "dma_start", out=dst_all, in_=Us[:, :, tok0:tok0 + T].rearrange("c p t -> p c t"),
             reads=[dbuf("Us", s, ti)], writes=[dst_buf], dsem=(ds or ds_u))

    def p2a_load(s, ti):
        load_uT(s, ti, uTs_all[ti % 2], b_uTs[ti % 2], ds_u2[ti % 2])

    def p2b_load(s, ti):
        load_uT(s, ti, uTs_all[0], b_uTs[0], ds_u2[0])

    def proj_chunk(wv, bsl, m, bank, uTl, buT):
        for kc in range(NCH):
            S.op("pe", "matmul", psum[bank], lhsT=wv[:, kc, m * 128:(m + 1) * 128], rhs=uTl[kc], start=(kc == 0), stop=(kc == NCH - 1),
                 reads=[bsl, buT], writes=[b_ps[bank]])

    def evac_perm(src_bank, g, h, ti, eng, is_q):
        src = psum[src_bank]
        if is_q:
            dst = Qt[g][h]
            if g == 0:
                o_ap, i_ap = dst, src
            else:
                r = 4 if g == 1 else 16
                o_ap, i_ap = dst.rearrange("p (r j) -> p r j", r=r), src.rearrange("p (j r) -> p r j", r=r)
            wbufs = [b_Q[g][h]]
        else:
            dst = KT[g][h]
            if g == 0:
                o_ap, i_ap = dst[:, ti * T:(ti + 1) * T], src
            else:
                r = 4 if g == 1 else 16
                w = T // r
                o_ap = dst.rearrange("p (r l) -> p r l", r=r)[:, :, ti * w:(ti + 1) * w]
                i_ap = src.rearrange("p (j r) -> p r j", r=r)
            wbufs = [b_KT[g][h]]
        if eng == "act":
            S.op("act", "copy", o_ap, i_ap, reads=[b_ps[src_bank]], writes=wbufs)
        else:
            S.op("dve", "tensor_copy", o_ap, i_ap, reads=[b_ps[src_bank]], writes=wbufs)

    def phase_p2a(s, ti, hook):
        tok0 = s * SEQ + ti * T
        uT, b_uT = uTs[ti % 2], b_uTs[ti % 2]
        hook()
        cnt = 0
        for g in range(3):
            sl, bsl = wget(("win", 9 + g))
            wv = sl.rearrange("p (c n) -> p c n", c=16)
            for hh in range(4):
                bank = cnt % 2
                cnt += 1
                proj_chunk(wv, bsl, hh, bank, uT, b_uT)
                evac_perm(bank, g, hh, ti, "act" if cnt % 2 else "dve", False)
        for g in range(3):
            sl, bsl = wget(("win", 12 + g))
            wv = sl.rearrange("p (c n) -> p c n", c=16)
            for b in range(4):
                bank = cnt % 2
                cnt += 1
                for kc in range(NCH):
                    S.op("pe", "matmul", psum[bank], lhsT=uT[kc][:, b * 128:(b + 1) * 128], rhs=wv[:, kc, :], start=(kc == 0), stop=(kc == NCH - 1),
                         reads=[bsl, b_uT], writes=[b_ps[bank]])
                if cnt % 2:
                    S.op("act", "copy", vst[b][:, g * 512:(g + 1) * 512], psum[bank], reads=[b_ps[bank]], writes=[b_vst[b]])
                else:
                    S.op("dve", "tensor_copy", vst[b][:, g * 512:(g + 1) * 512], psum[bank], reads=[b_ps[bank]], writes=[b_vst[b]])
        for b in range(4):
            store(Vs[tok0 + b * 128: tok0 + (b + 1) * 128, :], vst[b], [b_vst[b]], ("Vs", s, ti))

    def load_V(s):
        s0 = s * SEQ
        rd = [dbuf("Vs", s, ti) for ti in range(ntiles)]
        S.op("pool", "dma_start", out=Vg[0].rearrange("p (c n) -> p c n", c=16),
             in_=Vs[s0:s0 + SEQ, 0:512].rearrange("(c p) n -> p c n", p=128), reads=rd, writes=[b_V[0]], dsem=ds_v)
        V1d = Vg[1].rearrange("p (c r n) -> p c r n", c=4, r=4)
        for c in range(4):
            S.op("pool", "dma_start", out=V1d[:, c, :, :],
                 in_=Vs[s0 + c * 512:s0 + (c + 1) * 512, 512:1024].rearrange("(p r) n -> p r n", r=4), reads=rd, writes=[b_V[1]], dsem=ds_v)
        S.op("pool", "dma_start", out=Vg[2].rearrange("p (r n) -> p r n", r=16),
             in_=Vs[s0:s0 + SEQ, 1024:1536].rearrange("(p r) n -> p r n", r=16), reads=rd, writes=[b_V[2]], dsem=ds_v)

    SCALE = float(128 ** -0.5)

    def etile(g, h):
        if g < 2:
            base = ((g * 4 + h) * 3) * 128
            return etab[:, base: base + 384]
        base = (24 + h) * 128
        return etab[:, base: base + 128]

    def phase_p2b(s, ti, hook):
        tok0 = s * SEQ + ti * T
        uT, b_uT = uTs[0], b_uTs[0]
        cnt = 0
        for g in range(3):
            sl, bsl = wget(("win", 6 + g))
            wv = sl.rearrange("p (c n) -> p c n", c=16)
            for hh in range(4):
                bank = cnt % 2
                cnt += 1
                proj_chunk(wv, bsl, hh, bank, uT, b_uT)
                evac_perm(bank, g, hh, ti, "act" if cnt % 2 else "dve", True)
        hook()
        nblk = SEQ // 128
        V0 = Vg[0].rearrange("p (c n) -> p c n", c=16)
        V1 = Vg[1].rearrange("p (c r n) -> p c r n", c=4, r=4)
        V2 = Vg[2].rearrange("p (r n) -> p r n", r=16)
        blocks = []
        for hh in range(4):
            ob = 4 + 2 * (hh % 2)
            lb = 5 + 2 * (hh % 2)
            hb = []
            K1 = KT[1][hh].rearrange("p (r l) -> p r l", r=4)
            Q1 = Qt[1][hh].rearrange("p (r j) -> p r j", r=4)
            O1 = psum[ob].rearrange("p (j r) -> p r j", r=4)
            L1 = psum[lb].rearrange("p (j r) -> p r j", r=4)
            K2 = KT[2][hh].rearrange("p (r l) -> p r l", r=16)
            Q2 = Qt[2][hh].rearrange("p (r j) -> p r j", r=16)
            O2 = psum[ob].rearrange("p (j r) -> p r j", r=16)
            L2 = psum[lb].rearrange("p (j r) -> p r j", r=16)
            for qb in range(4):
                B = 4 * ti + qb
                rels = [r for r in range(3) if 0 <= B - 1 + r < nblk]
                qk = [(slice(r * 128, (r + 1) * 128), KT[0][hh][:, (B - 1 + r) * 128:(B + r) * 128], Qt[0][hh][:, qb * 128:(qb + 1) * 128]) for r in rels]
                lo, hi = rels[0] * 128, (rels[-1] + 1) * 128
                pvs = [(psum[ob][:, qb * 128:(qb + 1) * 128], psum[lb][:, qb * 128:(qb + 1) * 128], V0[:, B - 1 + r, hh * 128:(hh + 1) * 128],
                        slice(r * 128, (r + 1) * 128)) for r in rels]
                hb.append(dict(qk=qk, kb=b_KT[0][hh], qbuf=b_Q[0][hh], lo=lo, hi=hi, e=etile(0, hh)[:, lo:hi], bc=None, pv=pvs, vb=b_V[0]))
            for rc in range(4):
                rels = [r for r in range(3) if 0 <= ti - 1 + r < 4]
                qk = [(slice(r * 128, (r + 1) * 128), K1[:, rc, (ti - 1 + r) * 128:(ti + r) * 128], Q1[:, rc, :]) for r in rels]
                lo, hi = rels[0] * 128, (rels[-1] + 1) * 128
                pvs = [(O1[:, rc, :], L1[:, rc, :], V1[:, ti - 1 + r, rc, hh * 128:(hh + 1) * 128], slice(r * 128, (r + 1) * 128)) for r in rels]
                hb.append(dict(qk=qk, kb=b_KT[1][hh], qbuf=b_Q[1][hh], lo=lo, hi=hi, e=etile(1, hh)[:, lo:hi], bc=None, pv=pvs, vb=b_V[1]))
            qk = [(slice(rc * 32, (rc + 1) * 32), K2[:, rc, :], Q2[:, rc, :]) for rc in range(16)]
            pvs = [(O2[:, rc, :], L2[:, rc, :], V2[:, rc, hh * 128:(hh + 1) * 128], slice(rc * 32, (rc + 1) * 32)) for rc in range(16)]
            hb.append(dict(qk=qk, kb=b_KT[2][hh], qbuf=b_Q[2][hh], lo=0, hi=512, e=etile(2, hh)[:, ti * 32:(ti + 1) * 32], bc=16, pv=pvs, vb=b_V[2]))
            for n_, blk in enumerate(hb):
                blk.update(hh=hh, ob=ob, lb=lb, first=(n_ == 0), last=(n_ == len(hb) - 1))
            blocks += hb

        def stage_a(n_):
            blk = blocks[n_]
            sb = n_ % 4
            for (csl, k_ap, q_ap) in blk["qk"]:
                S.op("pe", "matmul", psum[sb][:, csl], lhsT=k_ap, rhs=q_ap, start=True, stop=True,
                     reads=[blk["kb"], blk["qbuf"]], writes=[b_ps[sb]])

        def stage_b(n_):
            blk = blocks[n_]
            sb = n_ % 4
            i = n_ % 4
            lo, hi = blk["lo"], blk["hi"]
            S.op("act", "activation", ptmp[i][:, lo:hi], psum[sb][:, lo:hi], AF.Exp, scale=SCALE, reads=[b_ps[sb]], writes=[b_ptmp[i]])
            if blk["bc"] is None:
                S.op("dve", "tensor_tensor", pbf[i][:, lo:hi], ptmp[i][:, lo:hi], blk["e"], ALU.mult, reads=[b_ptmp[i], b_const], writes=[b_pbf[i]])
            else:
                r = blk["bc"]
                S.op("dve", "tensor_tensor", pbf[i].rearrange("p (r j) -> p r j", r=r), ptmp[i].rearrange("p (r j) -> p r j", r=r),
                     blk["e"].unsqueeze(1).broadcast_to([128, r, 512 // r]), ALU.mult, reads=[b_ptmp[i], b_const], writes=[b_pbf[i]])

        def stage_c(n_):
            blk = blocks[n_]
            i = n_ % 4
            ob, lb, hh = blk["ob"], blk["lb"], blk["hh"]
            for m_, (o_ap, l_ap, v_ap, psl) in enumerate(blk["pv"]):
                st = blk["first"] and m_ == 0
                S.op("pe", "matmul", o_ap, lhsT=v_ap, rhs=pbf[i][:, psl], start=st, stop=False, skip_group_check=True,
                     reads=[b_pbf[i], blk["vb"]], writes=[b_ps[ob]])
                S.op("pe", "matmul", l_ap, lhsT=ones, rhs=pbf[i][:, psl], start=st, stop=False, skip_group_check=True,
                     reads=[b_pbf[i], b_ones], writes=[b_ps[lb]])
            if blk["last"]:
                S.op("dve", "reciprocal", rl, psum[lb], reads=[b_ps[lb]], writes=[b_rl])
                oi = nxt("ob")
                S.op("dve", "tensor_tensor", ob16[oi], psum[ob], rl, ALU.mult, reads=[b_ps[ob], b_rl], writes=[b_ob16[oi]])
                store(Os[hh, :, tok0:tok0 + T], ob16[oi], [b_ob16[oi]], ("Os", s, ti))

        nb_ = len(blocks)
        stage_a(0)
        if nb_ > 1:
            stage_a(1)
        for n_ in range(nb_):
            stage_b(n_)
            if n_ + 2 < nb_:
                stage_a(n_ + 2)
            stage_c(n_)

    wc = region(PH0)
    o_uT2 = [wc(NCH * 1024) for i in range(2)]
    uT2s = [[vbf(o + c * 1024, 512) for c in range(NCH)] for o in o_uT2]
    uT2s_all = [vbf(o, NCH * 512).rearrange("p (c n) -> p c n", c=NCH) for o in o_uT2]
    b_uT2s = [Buf("uT2a"), Buf("uT2b")]
    uhs = [vbf(wc(NCH * 32 * 2), NCH * 32).rearrange("p (c n) -> p c n", c=NCH) for i in range(2)]
    b_uhs = [Buf("uh0"), Buf("uh1")]
    ds_uh = [new_dsem(), new_dsem()]
    ds_uc = [new_dsem(), new_dsem()]
    o_ot = wc(4 * 1024)
    otile = [vbf(o_ot + h * 1024, 512) for h in range(4)]
    otile_all = vbf(o_ot, 4 * 512).rearrange("p (c n) -> p c n", c=4)
    b_ot = Buf("otile")
    ds_ot = new_dsem()
    zf = [vf32(wc(516 * 4), 514) for i in range(2)]
    b_zf = [Buf("zf0"), Buf("zf1")]
    tcc = [vf32(wc(2048), 512) for i in range(2)]
    b_tcc = [Buf("tcc0"), Buf("tcc1")]
    th = vf32(wc(64), 4)
    b_th = Buf("th")
    cacc = [vf32(wc(2048), 512) for i in range(8)]
    b_cacc = [Buf("cacc%d" % i) for i in range(8)]
    o_yc = wc(8 * 1024)
    ycin = [vbf(o_yc + i * 1024, 512) for i in range(8)]
    b_yc = [Buf("yc%d" % i) for i in range(8)]
    mg = [vbf(wc(1024), 512) for c in range(NCH)]
    b_mg = [Buf("mg%d" % c) for c in range(NCH)]
    gt = [vf32(wc(2048), 512) for i in range(4)]
    b_gt = [Buf("gt%d" % i) for i in range(4)]
    stg2 = [vf32(wc(2048), 512) for c in range(NCH)]
    b_stg2 = [Buf("stgm%d" % c) for c in range(NCH)]
    ds_hr = [new_dsem() for i in range(8)]
    hres = cacc + [vf32(o_yc + j * 2048, 512) for j in range(4)] + gt
    hres_bufs = [[b_cacc[j]] for j in range(8)] + [[b_yc[2 * j], b_yc[2 * j + 1]] for j in range(4)] + [[b_gt[j]] for j in range(4)]
    sq2 = [vbf(wc(1024), 512) for i in range(2)]
    b_sq2 = [Buf("sqm0"), Buf("sqm1")]
    rstd2 = vf32(wc(2048), 512)
    b_rstd2 = Buf("rstdm")
    tt2 = [vf32(wc(2048), 512) for i in range(2)]
    b_tt2 = [Buf("ttm0"), Buf("ttm1")]

    def p2c_load(s, ti):
        tok0 = s * SEQ + ti * T
        k = ti % 2
        load_uT(s, ti, uT2s_all[k], b_uT2s[k], ds_uc[k])
        S.op("dve", "memset", uhs[k], 0.0, writes=[b_uhs[k]])
        if ti > 0:
            S.op("sp", "dma_start", out=uhs[k][:, :, 0:16], in_=Us[:, :, tok0 - 16:tok0].rearrange("c p t -> p c t"),
                 reads=[dbuf("Us", s, ti - 1)], writes=[b_uhs[k]], dsem=ds_uh[k])
        if ti < NT - 1:
            S.op("sp", "dma_start", out=uhs[k][:, :, 16:32], in_=Us[:, :, tok0 + T:tok0 + T + 16].rearrange("c p t -> p c t"),
                 reads=[dbuf("Us", s, ti + 1)], writes=[b_uhs[k]], dsem=ds_uh[k])

    def phase_p2c(s, ti, hook):
        tok0 = s * SEQ + ti * T
        k = ti % 2
        uT2, b_uT2, uh, b_uh = uT2s[k], b_uT2s[k], uhs[k], b_uhs[k]
        S.op("sp", "dma_start", out=otile_all, in_=Os[:, :, tok0:tok0 + T].rearrange("c p t -> p c t"),
             reads=[dbuf("Os", s, ti)], writes=[b_ot], dsem=ds_ot)
        hook()
        HL = 6
        for q in range(4):
            sx, bsx = wget(("ccx", q))
            vcc = sx[:, 0:4096].rearrange("p (c n) -> p c n", c=16)
            vcx = sx[:, 4096:8192].rearrange("p (c n) -> p c n", c=16)
            for m in range(2):
                i8 = 2 * q + m
                bcc_k = 0 + 2 * (i8 % 2)
                bcx_k = 1 + 2 * (i8 % 2)
                for (wv, bank, which) in ((vcc, bcc_k, 0), (vcx, bcx_k, 1)):
                    hcol = (i8 * 2 + which) * 2
                    for kc in range(NCH):
                        S.op("pe", "matmul", psum[bank], lhsT=wv[:, kc, m * 128:(m + 1) * 128], rhs=uT2[kc], start=(kc == 0), stop=(kc == NCH - 1),
                             reads=[bsx, b_uT2], writes=[b_ps[bank]])
                        S.op("pe", "matmul", psum[HL][:, hcol:hcol + 2], lhsT=wv[:, kc, m * 128:(m + 1) * 128], rhs=uh[:, kc, 15:17],
                             start=(kc == 0), stop=(kc == NCH - 1), reads=[bsx, b_uh], writes=[b_ps[HL]])
                zi = nxt("zf")
                ci = nxt("tcc")
                hc = (i8 * 2) * 2
                S.op("act", "copy", tcc[ci], psum[bcc_k], reads=[b_ps[bcc_k]], writes=[b_tcc[ci]])
                S.op("dve", "tensor_tensor", zf[zi][:, 1:513], tcc[ci], psum[bcx_k], ALU.mult, reads=[b_tcc[ci], b_ps[bcx_k]], writes=[b_zf[zi]])
                S.op("act", "copy", th[:, 0:2], psum[HL][:, hc:hc + 2], reads=[b_ps[HL]], writes=[b_th])
                S.op("dve", "tensor_tensor", zf[zi][:, 0:514:513], th[:, 0:2], psum[HL][:, hc + 2:hc + 4], ALU.mult,
                     reads=[b_th, b_ps[HL]], writes=[b_zf[zi]])
                S.op("dve", "tensor_scalar_mul", cacc[i8], zf[zi][:, 0:512], gcol(PV_CW0, i8), reads=[b_zf[zi], b_const], writes=[b_cacc[i8]])
                S.op("dve", "scalar_tensor_tensor", cacc[i8], zf[zi][:, 1:513], gcol(PV_CW1, i8), cacc[i8], ALU.mult, ALU.add,
                     reads=[b_zf[zi], b_cacc[i8], b_const], writes=[b_cacc[i8]])
                S.op("dve", "scalar_tensor_tensor", cacc[i8], zf[zi][:, 2:514], gcol(PV_CW2, i8), cacc[i8], ALU.mult, ALU.add,
                     reads=[b_zf[zi], b_cacc[i8], b_const], writes=[b_cacc[i8]])
        for hf in range(2):
            scb, bcb = wget(("win", hf))
            vcb = scb.rearrange("p (c n) -> p c n", c=16)
            for m in range(4):
                i8 = hf * 4 + m
                bank = 4 + (m % 2)
                proj_chunk(vcb, bcb, m, bank, uT2, b_uT2)
                S.op("dve", "scalar_tensor_tensor", ycin[i8], cacc[i8], gcol(PV_CB, i8), psum[bank], ALU.add, ALU.mult,
                     reads=[b_cacc[i8], b_ps[bank], b_const], writes=[b_yc[i8]])
        for q in range(8):
            sg, bsg = wget(("gc", q))
            vg = sg[:, 0:4096].rearrange("p (c n) -> p c n", c=16)
            vco = sg[:, 4096:4096 + 2048].rearrange("p (c n) -> p c n", c=8)
            for m in range(2):
                dch = 2 * q + m
                A = 0 + 2 * (dch % 2)
                C = 1 + 2 * (dch % 2)
                msl = slice(m * 128, (m + 1) * 128)
                for kc in range(8):
                    S.op("pe", "matmul", psum[A], lhsT=vco[:, kc, msl], rhs=ycin[kc], start=(kc == 0), stop=(kc == 7),
                         reads=[bsg, b_yc[kc]], writes=[b_ps[A]])
                proj_chunk(vg, bsg, m, C, uT2, b_uT2)
                i0 = nxt("gt", 4)
                S.op("act", "activation", gt[i0], psum[C], AF.Sigmoid, bias=gcol(PV_BG, dch), scale=1.0, reads=[b_ps[C], b_const], writes=[b_gt[i0]])
                S.op("dve", "tensor_tensor", stg2[dch], gt[i0], psum[A], ALU.mult, reads=[b_gt[i0], b_ps[A]], writes=[b_stg2[dch]])
        for q in range(8):
            sg, bsg = wget(("ga", q))
            vg = sg[:, 0:4096].rearrange("p (c n) -> p c n", c=16)
            vao = sg[:, 4096:4096 + 1024].rearrange("p (c n) -> p c n", c=4)
            for m in range(2):
                dch = 2 * q + m
                Bk = 4 + 2 * (dch % 2)
                Dk = 5 + 2 * (dch % 2)
                msl = slice(m * 128, (m + 1) * 128)
                for hh in range(4):
                    S.op("pe", "matmul", psum[Bk], lhsT=vao[:, hh, msl], rhs=otile[hh], start=(hh == 0), stop=(hh == 3),
                         reads=[bsg, b_ot], writes=[b_ps[Bk]])
                proj_chunk(vg, bsg, m, Dk, uT2, b_uT2)
                i1 = nxt("gt", 4)
                S.op("act", "activation", gt[i1], psum[Dk], AF.Sigmoid, bias=gcol(PV_BG, 16 + dch), scale=1.0, reads=[b_ps[Dk], b_const], writes=[b_gt[i1]])
                S.op("dve", "tensor_tensor", gt[i1], gt[i1], psum[Bk], ALU.mult, reads=[b_gt[i1], b_ps[Bk]], writes=[b_gt[i1]])
                S.op("pool", "tensor_tensor", mg[dch], stg2[dch], gt[i1], ALU.add, reads=[b_stg2[dch], b_gt[i1]], writes=[b_mg[dch]])
        for c in range(NCH):
            S.op("sp", "dma_start", out=hres[c], in_=Hs[c, :, tok0:tok0 + T], reads=[dbuf("Hs", s, ti)], writes=hres_bufs[c], dsem=ds_hr[c % 8])
        pend2 = None
        for g in range(4):
            swo, bwo = wget(("wo", g))
            vwo = swo.rearrange("p (c n) -> p c n", c=16)
            for m in range(4):
                d2 = 4 * g + m
                py = d2 % 2
                for kc in range(NCH):
                    S.op("pe", "matmul", psum[py], lhsT=vwo[:, kc, m * 128:(m + 1) * 128], rhs=mg[kc], start=(kc == 0), stop=(kc == NCH - 1),
                         reads=[bwo, b_mg[kc]], writes=[b_ps[py]])
                S.op("dve", "tensor_copy", stg2[d2], psum[py], reads=[b_ps[py]], writes=[b_stg2[d2]])
                si = stat_sq(psum[py], b_ps[py], sq2, b_sq2, "sq2")
                if pend2 is not None:
                    stat_mm(pend2[0], 2, pend2[1] == 0, False, sq2, b_sq2)
                pend2 = (si, d2)
        stat_mm(pend2[0], 2, False, True, sq2, b_sq2)
        finish_rstd(2, rstd2, b_rstd2)
        for c in range(NCH):
            i = nxt("tt2")
            S.op("dve", "scalar_tensor_tensor", tt2[i], stg2[c], gcol(PV_MPOST, c), rstd2, ALU.mult, ALU.mult,
                 reads=[b_stg2[c], b_rstd2, b_const], writes=[b_tt2[i]])
            S.op("pool" if c % 2 == 0 else "dve", "tensor_tensor", stg2[c], tt2[i], hres[c], ALU.add, reads=[b_tt2[i]] + hres_bufs[c], writes=[b_stg2[c]])
            store(H2s[c, :, tok0:tok0 + T], stg2[c], [b_stg2[c]], ("H2s", s, ti))

    def run_group(items, preloaded=False, chain=()):
        if not items:
            return
        if not preloaded:
            items[0][1](items[0][2], items[0][3])
        allit = list(items) + list(chain)
        for k, (fn, ld, s_, ti_) in enumerate(items):
            def hook(k=k):
                if k + 1 < len(allit):
                    nfn, nld, ns, nti = allit[k + 1]
                    nld(ns, nti)
            fn(s_, ti_, hook)

    first_p1 = True
    for s in range(nseq):
        if "P1" in phases:
            items = [(phase_p1, p1_load, s, ti) for ti in range(ntiles)]
            if first_p1:
                p1_load(s, 0)
                deferred_conv.pop(0)()
                if ntiles < 4:
                    for f in deferred_conv:
                        f()
                    deferred_conv.clear()
                run_group(items[:1], preloaded=True, chain=items[1:2])
                conv_ready[0] = True
                run_group(items[1:], preloaded=True)
                for f in deferred_conv:
                    f()
                deferred_conv.clear()
            else:
                run_group(items)
            first_p1 = False
        if "P2" in phases:
            S.barrier()
            run_group([(phase_p2a, p2a_load, s, ti) for ti in range(ntiles)])
            load_V(s)
            run_group([(phase_p2b, p2b_load, s, ti) for ti in range(ntiles)])
            S.barrier()
            run_group([(phase_p2c, p2c_load, s, ti) for ti in range(ntiles)])
            S.barrier()
        if "P3" in phases:
            run_group([(phase_p3, p3_load, s, ti) for ti in range(ntiles)])
    assert wstate["next_use"] == len(stream), (wstate, len(stream))
    S.barrier()
    S.emit(block, sems)
    import os
    if os.environ.get("MK_STATS"):
        print("MK_STATS ops/signals", S.stats, "max dsem", max(d.count for d in S.dsems), "ndsem", len(S.dsems), flush=True)
    es.close()
    return nc


def make_consts(inp):
    def col(v):
        v = np.asarray(v, np.float32).reshape(-1)
        return v.reshape(-1, 128).T
    pv = np.zeros((128, PV_N), np.float32)
    pv[:, PV_PRE1:PV_PRE1 + 16] = col(inp["ffn1_norm_pre"])
    pv[:, PV_POST1:PV_POST1 + 16] = col(inp["ffn1_norm_post"])
    pv[:, PV_MPRE:PV_MPRE + 16] = col(inp["mix_norm_pre"])
    pv[:, PV_MPOST:PV_MPOST + 16] = col(inp["mix_norm_post"])
    pv[:, PV_PRE2:PV_PRE2 + 16] = col(inp["ffn2_norm_pre"])
    pv[:, PV_POST2:PV_POST2 + 16] = col(inp["ffn2_norm_post"])
    cw = np.asarray(inp["conv_w"], np.float32).reshape(3, DCONV)
    pv[:, PV_CW0:PV_CW0 + 8] = col(cw[0])
    pv[:, PV_CW1:PV_CW1 + 8] = col(cw[1])
    pv[:, PV_CW2:PV_CW2 + 8] = col(cw[2])
    pv[:, PV_CB:PV_CB + 8] = col(inp["conv_b"])
    pv[:, PV_BG:PV_BG + 32] = col(inp["b_gate"])
    ident = np.eye(128, dtype=np.float32)
    i = np.arange(1, 13, dtype=np.float32)
    slopes = np.exp2(np.float32(-8.0) * i / np.float32(12)).astype(np.float32).reshape(3, 4)
    dil = (1, 4, 16)
    kk = np.arange(128)[:, None]
    qq = np.arange(128)[None, :]
    et = np.zeros((128, 28 * 128), np.float32)
    for g in range(2):
        for h in range(4):
            for rel in range(3):
                delta = np.abs((rel - 1) * 128 + kk - qq)
                e = np.exp(-(slopes[g, h] * np.float32(dil[g]) * delta.astype(np.float32))).astype(np.float32)
                e = np.where(delta <= 64, e, np.float32(0))
                base = ((g * 4 + h) * 3 + rel) * 128
                et[:, base:base + 128] = e
    for h in range(4):
        delta = np.abs(kk - qq)
        e = np.exp(-(slopes[2, h] * np.float32(16) * delta.astype(np.float32))).astype(np.float32)
        e = np.where(delta <= 64, e, np.float32(0))
        et[:, (24 + h) * 128:(25 + h) * 128] = e
    return pv, ident, et


_NC_CACHE = {}


def kernel(x_prompt, x_sample, ffn1_norm_pre, ffn1_w1, ffn1_w3, ffn1_w2, ffn1_norm_post,
           mix_norm_pre, w_in, conv_w, conv_b, w_conv_out, w_attn_out, w_gate, b_gate, w_o,
           mix_norm_post, ffn2_norm_pre, ffn2_w1, ffn2_w3, ffn2_w2, ffn2_norm_post):
    inp = dict(ffn1_norm_pre=ffn1_norm_pre, ffn1_norm_post=ffn1_norm_post, mix_norm_pre=mix_norm_pre,
               mix_norm_post=mix_norm_post, ffn2_norm_pre=ffn2_norm_pre, ffn2_norm_post=ffn2_norm_post,
               conv_w=conv_w, conv_b=conv_b, b_gate=b_gate)
    pv, ident, et = make_consts(inp)
    xp = np.asarray(x_prompt, np.float32)
    xs = np.asarray(x_sample, np.float32)
    nb_p, nb_s = xp.shape[0], xs.shape[0]
    seqs = [("p", i) for i in range(nb_p)] + [("s", i) for i in range(nb_s)]
    nreal = len(seqs)
    assert nreal <= NCORES * NSEQ_CORE
    slot_of = {}
    order = []
    for k in range(NSEQ_CORE):
        for c in range(NCORES):
            order.append(c * NSEQ_CORE + k)
    for n, sq_ in enumerate(seqs):
        slot_of[order[n]] = sq_

    def get(sq_):
        return xp[sq_[1]] if sq_[0] == "p" else xs[sq_[1]]

    w2d = dict(
        f1w1=np.asarray(ffn1_w1, np.float32)[0], f1w3=np.asarray(ffn1_w3, np.float32)[0], f1w2=np.asarray(ffn1_w2, np.float32)[0],
        win=np.asarray(w_in, np.float32)[0], wgate=np.asarray(w_gate, np.float32)[0], wco=np.asarray(w_conv_out, np.float32)[0],
        wao=np.asarray(w_attn_out, np.float32)[0], wo=np.asarray(w_o, np.float32)[0],
        f2w1=np.asarray(ffn2_w1, np.float32)[0], f2w3=np.asarray(ffn2_w3, np.float32)[0], f2w2=np.asarray(ffn2_w2, np.float32)[0],
    )
    in_maps = []
    for c in range(NCORES):
        xs_c = []
        for k in range(NSEQ_CORE):
            sl = c * NSEQ_CORE + k
            xs_c.append(get(slot_of.get(sl, seqs[0])))
        m = dict(w2d)
        m["x"] = np.ascontiguousarray(np.concatenate(xs_c, axis=0))
        m["pvec"] = pv
        m["ident"] = ident
        m["etab"] = et
        in_maps.append(m)
    if "nc" not in _NC_CACHE:
        _NC_CACHE["nc"] = build()
    res = run_bass_kernel_spmd(_NC_CACHE["nc"], in_maps, core_ids=list(range(NCORES)))
    yp = np.empty_like(xp)
    ys = np.empty_like(xs)
    for sl, sq_ in slot_of.items():
        c, k = divmod(sl, NSEQ_CORE)
        blk = res.results[c]["y"][k * SEQ:(k + 1) * SEQ]
        if sq_[0] == "p":
            yp[sq_[1]] = blk
        else:
            ys[sq_[1]] = blk
    return (yp, ys)
```
